# Optimizing a Trainium2 kernel written in Bass

```python
import jax, jax.numpy as jnp
from jax import lax
import numpy as np

D_MODEL = 2048
BATCH = 2
SEQ = 16384
DEPTH = 1

D_MIX = D_MODEL
D_ATTN = D_MIX // 2
D_GMLP = D_MIX - D_ATTN
HEAD_DIM = 128
N_HEADS = D_ATTN // HEAD_DIM
N_KV_HEADS = 2
GQA_GROUP = N_HEADS // N_KV_HEADS
WINDOW = 128
ATTN_BLOCK = 128
GMLP_GROUP_DIM = 128
N_GMLP_GROUPS = D_GMLP // GMLP_GROUP_DIM
GMLP_CHUNK = 128
D_FF = 5632
CONV_WIDTH = 3
EPS = 1e-6

Q_COLS = N_HEADS * HEAD_DIM
KV_COLS = N_KV_HEADS * HEAD_DIM
IN_COLS = Q_COLS + 2 * KV_COLS + 2 * D_GMLP

kernel_name = "hymba_style_window_gqa_chunked_gmlp_convffn_encoder"


def rmsnorm(x, g):
    xf = x.astype(jnp.float32)
    y = xf * lax.rsqrt(jnp.mean(xf * xf, axis=-1, keepdims=True) + EPS)
    return (y * g.astype(jnp.float32)).astype(x.dtype)


def alibi_slopes(n):
    return jnp.exp2(-8.0 * jnp.arange(1, n + 1, dtype=jnp.float32) / n)


def windowed_gqa(q, k, v, sink):
    B, S = q.shape[0], q.shape[1]
    nb = S // ATTN_BLOCK
    qb = q.reshape(B, nb, ATTN_BLOCK, N_KV_HEADS, GQA_GROUP, HEAD_DIM)
    pad = ((0, 0), (ATTN_BLOCK, ATTN_BLOCK), (0, 0), (0, 0))

    def band(t):
        tp = jnp.pad(t, pad).reshape(B, nb + 2, ATTN_BLOCK, N_KV_HEADS, HEAD_DIM)
        return jnp.concatenate([tp[:, :-2], tp[:, 1:-1], tp[:, 2:]], axis=2)

    kb, vb = band(k), band(v)
    s = jnp.einsum('bnqkgd,bnskd->bnkgqs', qb, kb).astype(jnp.float32) * (HEAD_DIM ** -0.5)

    qi = jnp.arange(ATTN_BLOCK)[:, None]
    sj = jnp.arange(3 * ATTN_BLOCK)[None, :]
    rel = sj - ATTN_BLOCK - qi
    key_pos = jnp.arange(nb)[:, None, None] * ATTN_BLOCK - ATTN_BLOCK + sj[None]
    valid = (jnp.abs(rel) <= WINDOW)[None] & (key_pos >= 0) & (key_pos < S)
    slopes = alibi_slopes(N_HEADS).reshape(N_KV_HEADS, GQA_GROUP)
    alibi = -slopes[:, :, None, None] * jnp.abs(rel).astype(jnp.float32)
    s = jnp.where(valid[None, :, None, None], s + alibi[None, None], -jnp.inf)

    sink_l = jnp.broadcast_to(
        sink.astype(jnp.float32).reshape(N_KV_HEADS, GQA_GROUP)[None, None, :, :, None, None],
        s.shape[:-1] + (1,))
    p = jax.nn.softmax(jnp.concatenate([s, sink_l], axis=-1), axis=-1)[..., :-1]
    o = jnp.einsum('bnkgqs,bnskd->bnqkgd', p.astype(vb.dtype), vb)
    return o.reshape(B, S, N_HEADS * HEAD_DIM)


def chunked_spatial_gating(z, v_norm_g, ws, b):
    B, S = z.shape[0], z.shape[1]
    z = jax.nn.gelu(z)
    u, v = z[..., :D_GMLP], z[..., D_GMLP:]
    v = v.reshape(B, S // GMLP_CHUNK, GMLP_CHUNK, N_GMLP_GROUPS, GMLP_GROUP_DIM)
    v = rmsnorm(v, v_norm_g.reshape(N_GMLP_GROUPS, GMLP_GROUP_DIM))
    v = jnp.einsum('hts,bnshc->bnthc', ws, v) + b.T[None, None, :, :, None]
    return u * v.reshape(B, S, D_GMLP)


def conv_ffn(h, w_up, conv_w, conv_b, w_down):
    a = h @ w_up
    c = a.shape[-1]
    a = lax.conv_general_dilated(
        a, conv_w[:, None, :].astype(a.dtype), window_strides=(1,),
        padding=((CONV_WIDTH // 2, CONV_WIDTH // 2),),
        dimension_numbers=('NWC', 'WIO', 'NWC'), feature_group_count=c) + conv_b
    g, u = a[..., :D_FF], a[..., D_FF:]
    return (jax.nn.silu(g) * u) @ w_down


def setup_inputs(seed: int = 0) -> dict:
    key = jax.random.key(seed)
    ks = jax.random.split(key, 16)
    f32 = jnp.float32
    nrm = lambda k, shape, s: jax.random.normal(k, shape, f32) * s
    L = DEPTH
    return {
        "x": jax.random.normal(ks[0], (BATCH, SEQ, D_MODEL), f32),
        "norm1_g": 1.0 + nrm(ks[1], (L, D_MODEL), 0.02),
        "w_in": nrm(ks[2], (L, D_MODEL, IN_COLS), D_MODEL ** -0.5),
        "gmlp_v_norm_g": 1.0 + nrm(ks[3], (L, D_GMLP), 0.02),
        "gmlp_ws": nrm(ks[4], (L, N_GMLP_GROUPS, GMLP_CHUNK, GMLP_CHUNK), GMLP_CHUNK ** -0.5),
        "gmlp_b": 1.0 + nrm(ks[5], (L, N_GMLP_GROUPS, GMLP_CHUNK), 0.1),
        "attn_sink": nrm(ks[6], (L, N_HEADS), 0.5),
        "attn_out_norm_g": 1.0 + nrm(ks[7], (L, D_ATTN), 0.02),
        "gmlp_out_norm_g": 1.0 + nrm(ks[8], (L, D_GMLP), 0.02),
        "w_out": nrm(ks[9], (L, D_MIX, D_MODEL), D_MIX ** -0.5),
        "norm2_g": 1.0 + nrm(ks[10], (L, D_MODEL), 0.02),
        "w_up": nrm(ks[11], (L, D_MODEL, 2 * D_FF), D_MODEL ** -0.5),
        "conv_w": nrm(ks[12], (L, CONV_WIDTH, 2 * D_FF), CONV_WIDTH ** -0.5),
        "conv_b": nrm(ks[13], (L, 2 * D_FF), 0.01),
        "w_down": nrm(ks[14], (L, D_FF, D_MODEL), D_FF ** -0.5),
        "final_g": 1.0 + nrm(ks[15], (D_MODEL,), 0.02),
    }


def reference(x, norm1_g, w_in, gmlp_v_norm_g, gmlp_ws, gmlp_b, attn_sink,
              attn_out_norm_g, gmlp_out_norm_g, w_out, norm2_g, w_up, conv_w,
              conv_b, w_down, final_g):
    B, S = x.shape[0], x.shape[1]
    for l in range(DEPTH):
        h = rmsnorm(x, norm1_g[l])
        z = h @ w_in[l]
        q = z[..., :Q_COLS].reshape(B, S, N_HEADS, HEAD_DIM)
        k = z[..., Q_COLS:Q_COLS + KV_COLS].reshape(B, S, N_KV_HEADS, HEAD_DIM)
        v = z[..., Q_COLS + KV_COLS:Q_COLS + 2 * KV_COLS].reshape(B, S, N_KV_HEADS, HEAD_DIM)
        zg = z[..., Q_COLS + 2 * KV_COLS:]
        attn = windowed_gqa(q, k, v, attn_sink[l])
        gm = chunked_spatial_gating(zg, gmlp_v_norm_g[l], gmlp_ws[l], gmlp_b[l])
        mix = jnp.concatenate([rmsnorm(attn, attn_out_norm_g[l]),
                               rmsnorm(gm, gmlp_out_norm_g[l])], axis=-1)
        x = x + mix @ w_out[l]
        x = x + conv_ffn(rmsnorm(x, norm2_g[l]), w_up[l], conv_w[l], conv_b[l], w_down[l])
    return rmsnorm(x, final_g)
```

```python
import contextlib
import math
import numpy as np
import concourse.bass as bass
import concourse.mybir as mybir
from concourse.bass_utils import run_bass_kernel_spmd

F32 = mybir.dt.float32
BF16 = mybir.dt.bfloat16
AF = mybir.ActivationFunctionType
ALU = mybir.AluOpType
AX = mybir.AxisListType

D = 2048
DFF = 5632
NCORE = 8
SEQ = 16384
OWN = 4096
NB = 37
TOK = NB * 128
NST = 9
EPS = 1e-6
OWN0 = 384
OWN1 = OWN0 + OWN
XS = 16
HS = 18
NWS = 4
OPT_P1OV = True
ENGS = ("pe", "act", "dve", "pool", "sp")


class Op:
    __slots__ = ("eng", "fn", "deps", "sig", "sigval", "dkey")

    def __init__(self, eng, fn, dkey=None):
        self.eng = eng
        self.fn = fn
        self.deps = []
        self.sig = False
        self.sigval = 0
        self.dkey = dkey


class Prog:
    def __init__(self, nc, same_engine_raw=True):
        self.nc = nc
        self.ops = {e: [] for e in ENGS}
        self.last_w = {}
        self.readers = {}
        self.regions = {}
        self.same_engine_raw = same_engine_raw
        self.dkeys = {}

    def add(self, eng, fn, reads=(), writes=(), dkey=None, touch=(), join=False):
        op = Op(eng, fn, dkey)
        deps = {}
        if join:
            for r in writes:
                self.last_w[r] = op
            self.ops[eng].append(op)
            return op
        for r in reads:
            w = self.last_w.get(r)
            if w is not None:
                deps[id(w)] = (w, True)
        for r in writes:
            w = self.last_w.get(r)
            if w is not None and id(w) not in deps:
                deps[id(w)] = (w, False)
            for rd in self.readers.get(r, ()):
                if id(rd) not in deps:
                    deps[id(rd)] = (rd, False)
        for (R, tag) in touch:
            st = self.regions.setdefault(R, {"tag": None, "cur": {}, "prev": {}})
            if st["tag"] != tag:
                st["prev"] = st["cur"]
                st["cur"] = {}
                st["tag"] = tag
            for d in st["prev"].values():
                if id(d) not in deps:
                    deps[id(d)] = (d, False)
            k = ("dma", id(op)) if dkey is not None else eng
            st["cur"][k] = op
        for d, raw in deps.values():
            if d is op:
                continue
            if d.eng == eng and d.dkey is None:
                if not (raw and self.same_engine_raw):
                    continue
            op.deps.append(d)
            d.sig = True
        for r in reads:
            self.readers.setdefault(r, []).append(op)
        for r in writes:
            self.last_w[r] = op
            self.readers[r] = []
        self.ops[eng].append(op)
        if dkey is not None:
            self.dkeys.setdefault(dkey, 0)
        return op

    def emit(self, final_waits=()):
        nc = self.nc
        cnt = {e: 0 for e in ENGS}
        dcnt = {k: 0 for k in self.dkeys}
        for e in ENGS:
            for op in self.ops[e]:
                if op.dkey is not None:
                    dcnt[op.dkey] += 16
                    op.sigval = dcnt[op.dkey]
                elif op.sig:
                    cnt[e] += 1
                    op.sigval = cnt[e]
        with contextlib.ExitStack() as st:
            esem = {e: st.enter_context(nc.semaphore("s_" + e)) for e in ENGS}
            dsem = {k: st.enter_context(nc.semaphore("d_%d" % i))
                    for i, k in enumerate(self.dkeys)}
            blk = st.enter_context(nc.Block())

            def run(ename, eng):
                waited = {}
                for op in self.ops[ename]:
                    need = {}
                    for d in op.deps:
                        s = dsem[d.dkey] if d.dkey is not None else esem[d.eng]
                        k = id(s)
                        if k not in need or need[k][1] < d.sigval:
                            need[k] = (s, d.sigval)
                    for k, (s, v) in need.items():
                        if waited.get(k, 0) < v:
                            eng.wait_ge(s, v)
                            waited[k] = v
                    ins = op.fn(eng)
                    if op.dkey is not None:
                        ins.then_inc(dsem[op.dkey], 16)
                    elif op.sig:
                        ins.then_inc(esem[ename], 1)
                if ename == "sp":
                    for k in final_waits:
                        eng.wait_ge(dsem[k], dcnt[k])

            @blk.tensor
            def _(e):
                run("pe", e)

            @blk.scalar
            def _(e):
                run("act", e)

            @blk.vector
            def _(e):
                run("dve", e)

            @blk.gpsimd
            def _(e):
                run("pool", e)

            @blk.sync
            def _(e):
                run("sp", e)


def alibi_slopes(n):
    return [2.0 ** (-8.0 * (h + 1) / n) for h in range(n)]


def ffn_ranges():
    rs = []
    s = OWN0
    for j in range(NST):
        e = min(s + 510, 512 * j + 511, OWN1)
        rs.append((s, e))
        s = e
    assert s == OWN1
    return rs


def stage_panels(j):
    pl = [("win", 4), ("win", 5)]
    pl += [("win", i) for i in range(0, 4)]
    pl += [("win", 8), ("win", 9)]
    pl += [("win", i) for i in range(10, 14)]
    pl += [("win", 6), ("win", 7)]
    pl += [("wout", i) for i in range(8)]
    pl += [("wup", i) for i in range(44)]
    pl += [("wdn", i) for i in range(32)]
    return pl


def build_nc(debug=None):
    nc = bass.Bass("TRN2", target_bir_lowering=False)
    dt_in = lambda n, s, d=F32: nc.dram_tensor(n, s, d, kind="ExternalInput").ap()
    xT = dt_in("xT", [16, 128, TOK])
    w_in = dt_in("w_in", [D, 3584])
    w_out = dt_in("w_out", [D, D])
    w_up = dt_in("w_up", [D, 2 * DFF])
    w_down = dt_in("w_down", [DFF, D])
    g1_d = dt_in("g1", [128, 16])
    g2_d = dt_in("g2", [128, 16])
    gf_d = dt_in("gf", [128, 16])
    ga_d = dt_in("ga", [128, 8])
    gg_d = dt_in("gg", [128, 8])
    convw_d = dt_in("convw", [128, 3 * 88])
    convb_d = dt_in("convb", [128, 88])
    kvalid_d = dt_in("kvalid", [128, NB])
    sink_d = dt_in("sinkrep", [128, 8])
    gvg_d = dt_in("gvg", [128, 1024])
    wsT_d = dt_in("wsT", [128, 1024])
    brow_d = dt_in("brow", [1, 1024])
    absrel_d = dt_in("absrel", [128, 384])
    mask_d = dt_in("mask01", [128, 384])
    outT = nc.dram_tensor("outT", [16, 128, OWN], F32, kind="ExternalOutput").ap()

    win_s = nc.dram_tensor("win_s", [14, 128, 16, 256], BF16, kind="Internal").ap()
    wout_s = nc.dram_tensor("wout_s", [8, 128, 16, 256], BF16, kind="Internal").ap()
    wup_s = nc.dram_tensor("wup_s", [44, 128, 16, 256], BF16, kind="Internal").ap()
    wdn_s = nc.dram_tensor("wdn_s", [32, 128, 22, 128], BF16, kind="Internal").ap()
    scratch = {"win": win_s, "wout": wout_s, "wup": wup_s, "wdn": wdn_s}

    dbg_out = {}
    P = Prog(nc)
    slopes = alibi_slopes(8)
    FR = ffn_ranges()

    with contextlib.ExitStack() as st:
        def sb(name, shape, dt):
            return st.enter_context(nc.sbuf_tensor("sb_" + name, shape, dt))

        g1 = sb("g1", [128, 16], F32)
        g2 = sb("g2", [128, 16], F32)
        gf = sb("gf", [128, 16], F32)
        ga = sb("ga", [128, 8], F32)
        gg = sb("gg", [128, 8], F32)
        convw = sb("convw", [128, 3 * 88], F32)
        convb = sb("convb", [128, 88], F32)
        kvalid = sb("kvalid", [128, NB], F32)
        sinkrep = sb("sinkrep", [128, 8], F32)
        gvg = sb("gvg", [128, 1024], F32)
        wsT = sb("wsT", [128, 1024], BF16)
        bbias = sb("bbias", [2, 1024], BF16)
        bhi = bbias[0:1, :]
        blo = bbias[1:2, :]
        ones = sb("ones", [128, 128], BF16)
        dtab = sb("dtab", [128, 3 * 8 * 128], F32)
        sinkbc = sb("sinkbc", [128, 1024], F32)
        X1 = sb("X1", [128, 16, XS + 512], F32)
        xring = [sb("xr%d" % i, [128, 640], F32) for i in range(3)]
        sqA = [sb("sqa%d" % i, [128, 640], BF16) for i in range(3)]
        sqring = [sb("sq%d" % i, [128, 512], BF16) for i in range(2)]
        hreg = sb("hreg", [128, 16 * 640], BF16)
        hwin = hreg[:, :].rearrange("p (c t) -> p c t", c=16)
        h2win = hreg[:, 0:16 * (HS + 512)].rearrange("p (c t) -> p c t", c=16)
        blo0 = hreg[0:1, 4096:5120]
        h2st = sb("h2st", [128, 16, HS], BF16)
        kring = sb("kring", [128, 2, 8, 128], BF16)
        vring = sb("vring", [128, 8, 256], BF16)
        S1 = sb("S1", [128, 24576], BF16)
        qT = S1[:, 0:4096].rearrange("p (h t) -> p h t", h=8)
        uT = S1[:, 0:8192].bitcast(F32).rearrange("p (h t) -> p h t", h=8)
        anT = S1[:, 8192:16384].bitcast(F32).rearrange("p (h t) -> p h t", h=8)
        mixT = S1[:, 16384:24576].rearrange("p (h t) -> p h t", h=16)
        actT = S1[:, 0:44 * 512].rearrange("p (h t) -> p h t", h=44)
        gvall4 = S1[:, 8192:16384].bitcast(F32).rearrange("p (b c) -> p b c", b=4)
        gvall = [gvall4[:, b_, :] for b_ in range(4)]
        gv = sb("gv", [128, 1024], F32)
        vn = sb("vn", [128, 4, 1024], BF16)
        Eb = [sb("E%d" % i, [128, 512], F32) for i in range(2)]
        Pb = [sb("P%d" % i, [128, 512], BF16) for i in range(6)]
        ct = [Eb[0], Eb[1], gv[:, 0:512], gv[:, 512:1024]]
        CTK = [("E", 0), ("E", 1), ("gvh", 0), ("gvh", 1)]
        GVK2 = [("gvh", 0), ("gvh", 1)]
        rstd = sb("rstd", [128, 640], F32)
        rstd2 = sb("rstd2", [128, 512], F32)
        den = rstd[:, 0:512]
        gss = sb("gss", [128, 8], F32)
        wslot = [sb("w%d" % i, [128, 4096], BF16) for i in range(NWS)]
        ps = st.enter_context(nc.psum_tensor("ps", [128, 8, 512], F32))

        psn = [0]
        nrot = [5]

        def bank():
            b = psn[0] % nrot[0]
            psn[0] += 1
            return b

        def R_(eng, fn, reads=(), writes=(), dkey=None, touch=(), join=False):
            return P.add(eng, fn, reads=reads, writes=writes, dkey=dkey, touch=touch, join=join)

        def dma(out, in_, reads, writes, dkey, eng="sp", touch=(), join=False):
            return R_(eng, lambda e: e.dma_start(out=out, in_=in_), reads, writes, dkey, touch, join)

        def mm(out, lhsT, rhs, start, stop, reads, writes, touch=()):
            return R_("pe", lambda e: e.matmul(out, lhsT=lhsT, rhs=rhs, start=start, stop=stop),
                      reads, writes, touch=touch)

        def act(out, in_, func, reads, writes, bias=0.0, scale=1.0, touch=()):
            return R_("act", lambda e: e.activation(out=out, in_=in_, func=func, bias=bias, scale=scale),
                      reads, writes, touch=touch)

        def tt(eng, out, in0, in1, op, reads, writes, touch=()):
            return R_(eng, lambda e: e.tensor_tensor(out=out, in0=in0, in1=in1, op=op), reads, writes, touch=touch)

        def stt(eng, out, in0, scalar, in1, op0, op1, reads, writes, touch=()):
            return R_(eng, lambda e: e.scalar_tensor_tensor(out=out, in0=in0, scalar=scalar, in1=in1,
                                                            op0=op0, op1=op1), reads, writes, touch=touch)

        def cp(eng, out, in_, reads, writes, touch=()):
            return R_(eng, lambda e: e.tensor_copy(out=out, in_=in_), reads, writes, touch=touch)

        def recip(out, in_, reads, writes):
            return R_("dve", lambda e: e.reciprocal(out=out, in_=in_), reads, writes)

        def dbg(name, ap, reads, shape, touch=()):
            if debug is None or name not in debug.get("names", ()):
                return
            t = nc.dram_tensor("dbg_" + name, list(shape), ap.dtype, kind="ExternalOutput").ap()
            dbg_out[name] = t
            dma(t, ap, reads, [("dbg", name)], ("dbg", name), touch=touch)

        w_in_v = w_in.rearrange("(kc p) n -> p kc n", p=128)
        w_out_v = w_out.rearrange("(kc p) n -> p kc n", p=128)
        w_up_v = w_up.rearrange("(kc p) n -> p kc n", p=128)
        w_dn_v = w_down.rearrange("(kc p) n -> p kc n", p=128)
        w_up_v4 = w_up.rearrange("(kc p) (h n) -> p kc h n", p=128, h=2)
        NP0 = len(stage_panels(0))

        def src_f32(kind, i):
            if kind == "win":
                return w_in_v[:, :, 256 * i:256 * i + 256]
            if kind == "wout":
                return w_out_v[:, :, 256 * i:256 * i + 256]
            if kind == "wup":
                return w_up_v4[:, :, :, 128 * i:128 * i + 128]
            m, half = i // 2, i % 2
            return w_dn_v[:, 22 * half:22 * half + 22, 128 * m:128 * m + 128]

        wlist = []
        for j in range(NST):
            wlist += stage_panels(j)
        wstate = {"issued": 0, "next": 0}

        def wview(slot, kind):
            if kind == "wdn":
                return wslot[slot][:, 0:22 * 128].rearrange("p (c n) -> p c n", c=22)
            return wslot[slot][:, :].rearrange("p (c n) -> p c n", c=16)

        def wmode(idx):
            jj, loc = divmod(idx, NP0)
            if loc < 22:
                return "cast+store" if jj == 0 else "scratch"
            grp = (loc - 22) % 3
            if jj == 0:
                return "cast"
            if jj <= 3:
                if grp < jj - 1:
                    return "scratch"
                return "cast+store" if grp == jj - 1 else "cast"
            return "scratch"

        def wissue(upto):
            while wstate["issued"] < min(upto, len(wlist)):
                idx = wstate["issued"]
                kind, i = wlist[idx]
                slot = idx % NWS
                mode = wmode(idx)
                if mode == "scratch":
                    dma(wview(slot, kind), scratch[kind][i], [("scr", kind, i)], [("w", slot)], ("w", slot))
                else:
                    if kind == "wup":
                        d4 = wslot[slot][:, :].rearrange("p (c h n) -> p c h n", c=16, h=2)
                        dma(d4[:, :, 0, :], w_up_v[:, :, 128 * i:128 * i + 128], [], [("w", slot)], ("w0", slot), eng="pool")
                        dma(d4[:, :, 1, :], w_up_v[:, :, DFF + 128 * i:DFF + 128 * i + 128], [], [("w", slot)],
                            ("w0", slot), eng="pool", join=True)
                    else:
                        dma(wview(slot, kind), src_f32(kind, i), [], [("w", slot)], ("w0", slot), eng="pool")
                    if mode == "cast+store":
                        dma(scratch[kind][i], wview(slot, kind), [("w", slot)], [("scr", kind, i)], ("ws", slot))
                wstate["issued"] += 1

        def wnext(kind, i):
            idx = wstate["next"]
            assert wlist[idx] == (kind, i), (wlist[idx], kind, i)
            wissue(idx + NWS)
            wstate["next"] += 1
            slot = idx % NWS
            return wview(slot, kind), ("w", slot)

        def ld(t, d, key):
            dma(t[:], d, [], [key], ("c", key))

        ld(g1, g1_d, "g1"); ld(g2, g2_d, "g2"); ld(gf, gf_d, "gf"); ld(ga, ga_d, "ga"); ld(gg, gg_d, "gg")
        ld(convw, convw_d, "convw"); ld(convb, convb_d, "convb"); ld(kvalid, kvalid_d, "kvalid")
        ld(sinkrep, sink_d, "sinkrep"); ld(gvg, gvg_d, "gvg")
        wstg = hreg[:, 0:2048].bitcast(F32)
        dma(wstg, wsT_d, [], ["wstg"], ("c", "wsT"), touch=[("hreg", "init")])
        cp("dve", wsT[:], wstg, ["wstg"], ["wsT"], touch=[("hreg", "init")])
        for hf in range(2):
            sl = slice(512 * hf, 512 * hf + 512)
            stg = X1[0:1, 2 + hf, 0:512]
            tmp = X1[0:1, 4 + hf, 0:512]
            dma(stg, brow_d[:, sl], [], [("x1", 2 + hf)], ("c", "b%d" % hf))
            cp("dve", bhi[:, sl], stg, [("x1", 2 + hf)], [("bhi", hf)])
            cp("dve", tmp, bhi[:, sl], [("bhi", hf)], [("x1", 4 + hf)])
            tt("dve", blo0[:, sl], stg, tmp, ALU.subtract, [("x1", 2 + hf), ("x1", 4 + hf)], [("blo0", hf)],
               touch=[("hreg", "init")])
            dma(blo[:, sl], blo0[:, sl], [("blo0", hf)], [("blo", hf)], ("c", "blo%d" % hf), touch=[("hreg", "init")])
        R_("dve", lambda e: e.memset(ones[:], 1.0), [], ["ones"])
        R_("dve", lambda e: e.memset(S1[:], 0.0), [], [], touch=[("S1a0", "init"), ("S1a1", "init"), ("S1b", "init"), ("S1c", "init")])
        arel = X1[:, 0, 0:384]
        amsk = X1[:, 1, 0:384]
        dma(arel, absrel_d, [], [("x1", 0)], ("c", "absrel"))
        dma(amsk, mask_d, [], [("x1", 1)], ("c", "mask"))
        dt4 = dtab[:, :].rearrange("p (k h q) -> p k h q", k=3, h=8)
        for kb in range(3):
            for h in range(8):
                act(dt4[:, kb, h, :], arel[:, 128 * kb:128 * kb + 128], AF.Exp, [("x1", 0)], [("dt", kb, h)],
                    scale=-slopes[h])
                tt("dve", dt4[:, kb, h, :], dt4[:, kb, h, :], amsk[:, 128 * kb:128 * kb + 128], ALU.mult,
                   [("dt", kb, h), ("x1", 1)], [("dt", kb, h)])
        sb3 = sinkbc[:, :].rearrange("p (h q) -> p h q", h=8)
        for h in range(8):
            act(sb3[:, h, :], arel[:, 0:128], AF.Exp, [("x1", 0), "sinkrep"], [("sinkbc", h)],
                bias=sinkrep[:, h:h + 1], scale=0.0)
        DT_ALL = [("dt", kb, h) for kb in range(3) for h in range(8)]

        xTv = xT.rearrange("c p t -> p c t")
        outTv = outT.rearrange("c p t -> p c t")
        xr_n = [0]
        sq_n = [0]
        sqa_n = [0]
        e_n = [0]
        p_n = [0]

        def rms_finish(psb, n, nfeat, dst, rd_extra):
            act(dst, ps[:, psb, 0:n], AF.Sqrt, [("ps", psb)], rd_extra, bias=EPS, scale=1.0 / nfeat)
            recip(dst, dst, rd_extra, rd_extra)

        P1A, P1B = 5, 6
        p1q = []
        pending_tail = [None]

        def p1_pass1(jj, c):
            cc0 = 512 * jj
            xs = xr_n[0] % 3
            xr_n[0] += 1
            sq = sqa_n[0] % 3
            sqa_n[0] += 1
            dma(xring[xs][:], xTv[:, c, cc0:cc0 + 640], [], [("xr", xs)], ("xr", xs))
            act(sqA[sq][:], xring[xs][:], AF.Square, [("xr", xs)], [("sqa", sq)])
            p1q.append((c, sq))

        def p1_flush_mm(final=False):
            while len(p1q) > (0 if final else 1):
                c, sq = p1q.pop(0)
                mm(ps[:, P1A, 0:512], ones[:], sqA[sq][:, 0:512], c == 0, c == 15, ["ones", ("sqa", sq)], [("ps", P1A)])
                mm(ps[:, P1B, 0:128], ones[:], sqA[sq][:, 512:640], c == 0, c == 15, ["ones", ("sqa", sq)], [("ps", P1B)])

        def p1_finish():
            act(rstd[:, 0:512], ps[:, P1A, 0:512], AF.Sqrt, [("ps", P1A)], ["rstd"], bias=EPS, scale=1.0 / D)
            act(rstd[:, 512:640], ps[:, P1B, 0:128], AF.Sqrt, [("ps", P1B)], ["rstd"], bias=EPS, scale=1.0 / D)
            recip(rstd[:], rstd[:], ["rstd"], ["rstd"])

        def p1_pass2(jj, c):
            cc0 = 512 * jj
            xs = xr_n[0] % 3
            xr_n[0] += 1
            dma(xring[xs][:], xTv[:, c, cc0:cc0 + 640], [], [("xr", xs)], ("xr", xs))
            stt("dve", hwin[:, c, :], xring[xs][:], g1[:, c:c + 1], rstd[:],
                ALU.mult, ALU.mult, [("xr", xs), "g1", "rstd"], [("h", c)], touch=[("hreg", ("hwin", jj))])

        for j in range(NST):
            c0 = 512 * j
            mixed = [p for p in range(4) if 2 <= 4 * j + p < 36]
            s_j, e_j = FR[j]
            n_out = e_j - s_j
            S_j = c0 - s_j if j > 0 else 0
            S_n = (512 * (j + 1) - e_j) if j < NST - 1 else 0
            tg = lambda name: (name, j)
            PL = "dve" if j <= 3 else "pool"

            if j == 0 or not OPT_P1OV:
                nrot[0] = 5
                for c in range(16):
                    p1_pass1(j, c)
                    p1_flush_mm()
                p1_flush_mm(final=True)
                p1_finish()
                for c in range(16):
                    p1_pass2(j, c)
            nrot[0] = 7
            HALL = [("h", c) for c in range(16)]
            isdbg = debug is not None and debug.get("stage") == j
            if isdbg:
                dbg("hwin", hreg[:, :], HALL, [128, 16 * 640], touch=[("hreg", tg("hwin"))])

            wv, wk = wnext("win", 4)
            for kv in range(2):
                b = bank()
                for kc in range(16):
                    mm(ps[:, b, :], wv[:, kc, 128 * kv:128 * kv + 128], hwin[:, kc, 128:640], kc == 0, kc == 15,
                       [wk, ("h", kc)], [("ps", b)], touch=[("hreg", tg("hwin"))])
                for i in range(4):
                    B = 4 * j + 1 + i
                    cp("dve", kring[:, kv, B % 8, :], ps[:, b, 128 * i:128 * i + 128], [("ps", b)], [("k", B % 8, kv)])
            wv, wk = wnext("win", 5)
            for i in range(4):
                B = 4 * j + 1 + i
                b = bank()
                for kc in range(16):
                    mm(ps[:, b, 0:256], hwin[:, kc, 128 * (i + 1):128 * (i + 2)], wv[:, kc, :], kc == 0, kc == 15,
                       [wk, ("h", kc)], [("ps", b)], touch=[("hreg", tg("hwin"))])
                R_("act", lambda e, b=b, B=B: e.copy(out=vring[:, B % 8, :], in_=ps[:, b, 0:256]),
                   [("ps", b)], [("v", B % 8)])

            pending_stash = None
            if pending_tail[0] is not None:
                pending_tail[0]()
                pending_stash = pending_tail[0].stash
                pending_tail[0] = None
            for pn in range(4):
                wv, wk = wnext("win", pn)
                for mm_ in range(2):
                    h = 2 * pn + mm_
                    b = bank()
                    for kc in range(16):
                        mm(ps[:, b, :], wv[:, kc, 128 * mm_:128 * mm_ + 128], hwin[:, kc, 0:512], kc == 0, kc == 15,
                           [wk, ("h", kc)], [("ps", b)], touch=[("hreg", tg("hwin"))])
                    if h % 2 == 0:
                        cp("dve", qT[:, h, :], ps[:, b, :], [("ps", b)], [("q", h)], touch=[("S1a0", tg("q"))])
                    else:
                        R_("act", lambda e, b=b, h=h: e.copy(out=qT[:, h, :], in_=ps[:, b, :]), [("ps", b)], [("q", h)],
                           touch=[("S1a0", tg("q"))])
            if isdbg:
                dbg("qT", S1[:, 0:4096], [("q", h) for h in range(8)], [128, 4096], touch=[("S1a0", tg("q"))])
                dbg("kring", kring[:, :, :, :].rearrange("p a b c -> p (a b c)"),
                    [("k", s, kv) for s in range(8) for kv in range(2)], [128, 2048])
                dbg("vring", vring[:, :, :].rearrange("p a b -> p (a b)"), [("v", s) for s in range(8)], [128, 2048])

            units = [(p, kv) for p in mixed for kv in range(2)]

            def u_region(g):
                return "S1a0" if g < 4 else "S1a1"

            def attn_head(p, kv):
                B = 4 * j + p
                pbs = []
                for kb in range(3):
                    keyB = B - 1 + kb
                    bS = bank()
                    mm(ps[:, bS, :].rearrange("p (h q) -> p h q", h=4), kring[:, kv, keyB % 8, :],
                       qT[:, 4 * kv:4 * kv + 4, 128 * p:128 * p + 128],
                       True, True, [("k", keyB % 8, kv)] + [("q", h) for h in range(4 * kv, 4 * kv + 4)], [("ps", bS)],
                       touch=[("S1a0", tg("q"))])
                    eb = e_n[0] % 2
                    e_n[0] += 1
                    pb = p_n[0] % 6
                    p_n[0] += 1
                    act(Eb[eb][:], ps[:, bS, :], AF.Exp, [("ps", bS)], [("E", eb)], scale=1.0 / math.sqrt(128.0))
                    dsl = dtab[:, (kb * 8 + 4 * kv) * 128:(kb * 8 + 4 * kv + 4) * 128]
                    if keyB in (1, 2, 35, 36):
                        stt("dve", Pb[pb][:], Eb[eb][:], kvalid[:, keyB:keyB + 1], dsl, ALU.mult, ALU.mult,
                            [("E", eb), "kvalid"] + DT_ALL, [("P", pb)])
                    else:
                        tt(PL, Pb[pb][:], Eb[eb][:], dsl, ALU.mult,
                           [("E", eb)] + DT_ALL, [("P", pb)])
                    pbs.append(pb)
                return (p, kv, B, pbs)

            def attn_tail(p, kv, B, pbs):
                bO, bD = bank(), bank()
                for kb in range(3):
                    keyB = B - 1 + kb
                    pb = pbs[kb]
                    mm(ps[:, bO, :], vring[:, keyB % 8, 128 * kv:128 * kv + 128], Pb[pb][:], kb == 0, kb == 2,
                       [("v", keyB % 8), ("P", pb)], [("ps", bO)])
                    mm(ps[:, bD, :], ones[:], Pb[pb][:], kb == 0, kb == 2, ["ones", ("P", pb)], [("ps", bD)])
                tt("dve", den[:], ps[:, bD, :], sinkbc[:, 512 * kv:512 * kv + 512], ALU.add,
                   [("ps", bD)] + [("sinkbc", h) for h in range(4 * kv, 4 * kv + 4)], ["rstd"])
                recip(den[:], den[:], ["rstd"], ["rstd"])
                tt("dve", anT[:, 4 * kv:4 * kv + 4, 128 * p:128 * p + 128],
                   ps[:, bO, :].rearrange("p (h q) -> p h q", h=4),
                   den[:].rearrange("p (h q) -> p h q", h=4), ALU.mult,
                   [("ps", bO), "rstd"], [("an", p, kv)], touch=[("S1b", tg("an"))])

            ufill = [(8, 0), (8, 1), (9, 0), (9, 1)]
            uw = {}

            def u_gemm(pn, mm_):
                if pn not in uw:
                    uw[pn] = wnext("win", pn)
                wv, wk = uw[pn]
                g = 2 * (pn - 6) + mm_
                b = bank()
                for kc in range(16):
                    mm(ps[:, b, :], wv[:, kc, 128 * mm_:128 * mm_ + 128], hwin[:, kc, 0:512], kc == 0, kc == 15,
                       [wk, ("h", kc)], [("ps", b)], touch=[("hreg", tg("hwin"))])
                return g, b

            step = max(1, len(units) // 4)
            prev = None
            for ui, (p, kv) in enumerate(units):
                cur = attn_head(p, kv)
                if prev is not None:
                    attn_tail(*prev)
                prev = cur
                if ufill and ui % step == step - 1:
                    g, b = u_gemm(*ufill.pop(0))
                    cp("dve", uT[:, g, :], ps[:, b, :], [("ps", b)], [("u", g)], touch=[(u_region(g), tg("u"))])
            attn_tail(*prev)
            while ufill:
                g, b = u_gemm(*ufill.pop(0))
                cp("dve", uT[:, g, :], ps[:, b, :], [("ps", b)], [("u", g)], touch=[(u_region(g), tg("u"))])
            AN_ALL = [("an", p, kv) for (p, kv) in units]
            if isdbg:
                dbg("attn", S1[:, 8192:16384], AN_ALL, [128, 8192], touch=[("S1b", tg("an"))])

            def branch_norm(gvec, gkey, mix_off, tagname):
                bN = bank()
                for h in range(8):
                    sq = sq_n[0] % 2
                    sq_n[0] += 1
                    act(sqring[sq][:, 0:512], anT[:, h, :], AF.Square, AN_K, [("sq", sq)], touch=[("S1b", tg(tagname))])
                    mm(ps[:, bN, :], ones[:], sqring[sq][:, 0:512], h == 0, h == 7, ["ones", ("sq", sq)], [("ps", bN)])
                rms_finish(bN, 512, 1024, rstd2[:], ["rstd2"])
                for h in range(8):
                    stt("dve", mixT[:, mix_off + h, :], anT[:, h, :], gvec[:, h:h + 1], rstd2[:],
                        ALU.mult, ALU.mult, AN_K + [gkey, "rstd2"], [("mix", mix_off + h)],
                        touch=[("S1b", tg(tagname)), ("S1c", tg("mix"))])

            AN_K = AN_ALL
            branch_norm(ga, "ga", 0, "an")

            if pending_stash is not None:
                pending_stash()
            dma(X1[:, :, XS:XS + 512], xTv[:, :, c0:c0 + 512],
                [], [("x1", m) for m in range(16)], ("x1",))

            for pn in range(10, 14):
                wv, wk = wnext("win", pn)
                for p in mixed:
                    b = bank()
                    for kc in range(16):
                        mm(ps[:, b, 0:256], hwin[:, kc, 128 * p:128 * p + 128], wv[:, kc, :], kc == 0, kc == 15,
                           [wk, ("h", kc)], [("ps", b)], touch=[("hreg", tg("hwin"))])
                    act(gvall[p][:, 256 * (pn - 10):256 * (pn - 10) + 256], ps[:, b, 0:256], AF.Gelu_apprx_tanh,
                        [("ps", b)], [("gvall", p, pn - 10)], touch=[("S1b", tg("gvall"))])
            for p in mixed:
                GVK = [("gvall", p, q) for q in range(4)]
                tt("dve", gv[:], gvall[p][:], gvall[p][:], ALU.mult, GVK, GVK2, touch=[("S1b", tg("gvall"))])
                R_("dve", lambda e: e.tensor_reduce(out=gss[:], in_=gv[:].rearrange("p (g c) -> p g c", g=8),
                                                   axis=AX.X, op=ALU.add), GVK2, ["gss"])
                act(gss[:], gss[:], AF.Sqrt, ["gss"], ["gss"], bias=EPS, scale=1.0 / 128.0)
                recip(gss[:], gss[:], ["gss"], ["gss"])
                for g in range(8):
                    stt("dve", vn[:, p, 128 * g:128 * g + 128], gvall[p][:, 128 * g:128 * g + 128], gss[:, g:g + 1],
                        gvg[:, 128 * g:128 * g + 128], ALU.mult, ALU.mult, GVK + ["gss", "gvg"], [("vn", p, g)],
                        touch=[("S1b", tg("gvall"))])
            for (pn, mm_) in ((6, 0), (6, 1), (7, 0), (7, 1)):
                g, b = u_gemm(pn, mm_)
                act(uT[:, g, :], ps[:, b, :], AF.Gelu_apprx_tanh, [("ps", b)], [("u", g)], touch=[(u_region(g), tg("u"))])
            for g in range(4, 8):
                act(uT[:, g, :], uT[:, g, :], AF.Gelu_apprx_tanh, [("u", g)], [("u", g)], touch=[(u_region(g), tg("u"))])
            for g in range(8):
                b = bank()
                for p in mixed:
                    cs = slice(128 * p, 128 * p + 128)
                    mm(ps[:, b, cs], vn[:, p, 128 * g:128 * g + 128], wsT[:, 128 * g:128 * g + 128], True, False,
                       [("vn", p, g), "wsT"], [("ps", b)])
                    mm(ps[:, b, cs], ones[0:2, :], bbias[0:2, 128 * g:128 * g + 128], False, True,
                       ["ones", ("bhi", g // 4), ("blo", g // 4)], [("ps", b)])
                lo, hi = 128 * mixed[0], 128 * mixed[-1] + 128
                tt("dve", anT[:, g, lo:hi], ps[:, b, lo:hi], uT[:, g, lo:hi], ALU.mult, [("ps", b), ("u", g)], [("gm", g)],
                   touch=[("S1b", tg("gm")), (u_region(g), tg("u"))])
            AN_K = [("gm", g) for g in range(8)]
            if isdbg:
                dbg("gm", S1[:, 8192:16384], AN_K, [128, 8192], touch=[("S1b", tg("gm"))])
            branch_norm(gg, "gg", 8, "gm")
            MIX_ALL = [("mix", h) for h in range(16)]
            if isdbg:
                dbg("mixT", S1[:, 16384:24576], MIX_ALL, [128, 8192], touch=[("S1c", tg("mix"))])

            bN = 7
            pend_n = None
            for pn in range(8):
                wv, wk = wnext("wout", pn)
                for mm_ in range(2):
                    m = 2 * pn + mm_
                    b = bank()
                    for kc in range(16):
                        mm(ps[:, b, :], wv[:, kc, 128 * mm_:128 * mm_ + 128], mixT[:, kc, :], kc == 0, kc == 15,
                           [wk, ("mix", kc)], [("ps", b)], touch=[("S1c", tg("mix"))])
                    tt("dve", X1[:, m, XS:XS + 512], X1[:, m, XS:XS + 512], ps[:, b, :], ALU.add,
                       [("ps", b), ("x1", m)], [("x1", m)])
                    sq = sq_n[0] % 2
                    sq_n[0] += 1
                    act(sqring[sq][:, 0:512], X1[:, m, XS:XS + 512], AF.Square, [("x1", m)], [("sq", sq)])
                    if pend_n is not None:
                        pm, psq = pend_n
                        mm(ps[:, bN, :], ones[:], sqring[psq][:, 0:512], pm == 0, False, ["ones", ("sq", psq)], [("ps", bN)])
                    pend_n = (m, sq)
            pm, psq = pend_n
            mm(ps[:, bN, :], ones[:], sqring[psq][:, 0:512], False, True, ["ones", ("sq", psq)], [("ps", bN)])
            rms_finish(bN, 512, D, rstd2[:], ["rstd2"])
            if j == 0:
                R_("dve", lambda e: e.tensor_scalar(out=rstd2[:, 256:384], in0=rstd2[:, 256:384], scalar1=kvalid[:, 2:3],
                                                   scalar2=None, op0=ALU.mult), ["rstd2", "kvalid"], ["rstd2"])
            if j == NST - 1:
                R_("dve", lambda e: e.tensor_scalar(out=rstd2[:, 384:512], in0=rstd2[:, 384:512], scalar1=kvalid[:, 35:36],
                                                   scalar2=None, op0=ALU.mult), ["rstd2", "kvalid"], ["rstd2"])
            for m in range(16):
                stt("dve", h2win[:, m, HS:HS + 512], X1[:, m, XS:XS + 512], g2[:, m:m + 1],
                    rstd2[:], ALU.mult, ALU.mult, [("x1", m), "g2", "rstd2"], [("h2", m)], touch=[("hreg", tg("h2win"))])
            H2ALL = [("h2", m) for m in range(16)]
            if j > 0:
                cp(PL, h2win[:, :, HS - S_j - 1:HS], h2st[:, :, 0:S_j + 1], ["h2st"], ["h2s"],
                   touch=[("hreg", tg("h2win"))])
            if j < NST - 1:
                cp(PL, h2st[:, :, 0:S_n + 1], h2win[:, :, HS + 512 - S_n - 1:HS + 512], H2ALL, ["h2st"],
                   touch=[("hreg", tg("h2win"))])
            if isdbg:
                dbg("x1", X1[:, :, :].rearrange("p a b -> p (a b)"), [("x1", m) for m in range(16)], [128, 16 * (XS + 512)])

            na = n_out + 2
            ca = HS + (s_j - 1 - c0)
            for i in range(44):
                wv, wk = wnext("wup", i)
                bG, bU = bank(), bank()
                for (bb, off) in ((bG, 0), (bU, 128)):
                    for kc in range(16):
                        mm(ps[:, bb, 0:na], wv[:, kc, off:off + 128], h2win[:, kc, ca:ca + na], kc == 0, kc == 15,
                           [wk, ("h2", kc), "h2s"], [("ps", bb)], touch=[("hreg", tg("h2win"))])
                t1 = ct[(2 * i) % 4]
                t2 = ct[(2 * i + 1) % 4]
                k1 = CTK[(2 * i) % 4]
                k2 = CTK[(2 * i + 1) % 4]
                for (bb, tt_, kk, ch) in ((bG, t1, k1, i), (bU, t2, k2, 44 + i)):
                    tt_ = tt_ if (2 * i) % 4 == 0 else tt_
                    act(tt_[:, 0:n_out], ps[:, bb, 1:n_out + 1], AF.Identity, [("ps", bb), "convw", "convb"], [kk],
                        bias=convb[:, ch:ch + 1], scale=convw[:, 88 + ch:88 + ch + 1])
                    stt("dve", tt_[:, 0:n_out], ps[:, bb, 0:n_out], convw[:, ch:ch + 1], tt_[:, 0:n_out],
                        ALU.mult, ALU.add, [("ps", bb), kk, "convw"], [kk])
                    stt("dve", tt_[:, 0:n_out], ps[:, bb, 2:n_out + 2], convw[:, 176 + ch:176 + ch + 1], tt_[:, 0:n_out],
                        ALU.mult, ALU.add, [("ps", bb), kk, "convw"], [kk])
                act(t1[:, 0:n_out], t1[:, 0:n_out], AF.Silu, [k1], [k1])
                rg = "S1a0" if i < 8 else ("S1a1" if i < 16 else ("S1b" if i < 32 else "S1c"))
                tt(PL, actT[:, i, 0:n_out], t1[:, 0:n_out], t2[:, 0:n_out], ALU.mult, [k1, k2], [("act", i)],
                   touch=[(rg, tg("act"))])
            if isdbg:
                dbg("actT", S1[:, 0:44 * 512], [("act", i) for i in range(44)], [128, 44 * 512], touch=[("S1a0", tg("act")), ("S1a1", tg("act")), ("S1b", tg("act")), ("S1c", tg("act"))])

            cx = XS + (s_j - c0)
            bF = 7
            pend_f = None
            nrot[0] = 5
            for m in range(16):
                b = bank()
                for half in range(2):
                    wv, wk = wnext("wdn", 2 * m + half)
                    for kc in range(22):
                        ch = 22 * half + kc
                        rg = "S1a0" if ch < 8 else ("S1a1" if ch < 16 else ("S1b" if ch < 32 else "S1c"))
                        mm(ps[:, b, 0:n_out], wv[:, kc, :], actT[:, ch, 0:n_out], ch == 0, ch == 43,
                           [wk, ("act", ch)], [("ps", b)], touch=[(rg, tg("act"))])
                keys = [("x1", m)] + (["x1s"] if j > 0 else [])
                tt("dve", X1[:, m, cx:cx + n_out], X1[:, m, cx:cx + n_out], ps[:, b, 0:n_out], ALU.add,
                   [("ps", b)] + keys, keys)
                sq = sq_n[0] % 2
                sq_n[0] += 1
                act(sqring[sq][:, 0:n_out], X1[:, m, cx:cx + n_out], AF.Square, keys, [("sq", sq)])
                if j < NST - 1 and OPT_P1OV:
                    if m < 8:
                        p1_flush_mm()
                        p1_pass1(j + 1, 2 * m)
                        p1_pass1(j + 1, 2 * m + 1)
                    else:
                        if m == 8:
                            p1_flush_mm(final=True)
                            p1_finish()
                        p1_pass2(j + 1, 2 * (m - 8))
                        p1_pass2(j + 1, 2 * (m - 8) + 1)
                if pend_f is not None:
                    pm, psq = pend_f
                    mm(ps[:, bF, 0:n_out], ones[:], sqring[psq][:, 0:n_out], pm == 0, False, ["ones", ("sq", psq)], [("ps", bF)])
                pend_f = (m, sq)
            pm, psq = pend_f
            mm(ps[:, bF, 0:n_out], ones[:], sqring[psq][:, 0:n_out], False, True, ["ones", ("sq", psq)], [("ps", bF)])
            def make_tail(j=j, cx=cx, n_out=n_out, s_j=s_j, e_j=e_j, S_n=S_n):
                def tail():
                    rms_finish(7, n_out, D, rstd2[:, 0:n_out], ["rstd2"])
                    allk = [("x1", m) for m in range(16)] + ["x1s"]
                    for m in range(16):
                        keys = [("x1", m)] + (["x1s"] if j > 0 else [])
                        stt("dve", X1[:, m, cx:cx + n_out], X1[:, m, cx:cx + n_out], gf[:, m:m + 1],
                            rstd2[:, 0:n_out], ALU.mult, ALU.mult, keys + ["gf", "rstd2"], keys)
                    dma(outTv[:, :, s_j - OWN0:e_j - OWN0], X1[:, :, cx:cx + n_out], allk, ["outT"], ("out",))

                def stash():
                    allk = [("x1", m) for m in range(16)] + ["x1s"]
                    R_("sp", lambda e, a=X1[:, :, XS - S_n:XS], b_=X1[:, :, XS + 512 - S_n:XS + 512]:
                       e.dma_start(out=a, in_=b_, allow_slow_non_contiguous=True), allk, ["x1s"], ("x1st",))
                tail.stash = stash if j < NST - 1 else None
                return tail

            pending_tail[0] = make_tail()
            if j == NST - 1:
                pending_tail[0]()
                pending_tail[0] = None

        P.emit(final_waits=[("out",)] + [("dbg", n) for n in dbg_out])
    return nc, dbg_out


def _core_inputs(c, x, shared):
    bi, seg = divmod(c, NCORE // 2)
    s0 = seg * OWN
    g0 = s0 - OWN0
    xt = np.zeros((TOK, D), np.float32)
    lo, hi = max(g0, 0), min(g0 + TOK, SEQ)
    xt[lo - g0:hi - g0] = x[bi, lo:hi]
    xT = np.ascontiguousarray(xt.T).reshape(16, 128, TOK)
    kvalid = np.zeros((128, NB), np.float32)
    for B in range(NB):
        t = g0 + 128 * B
        kvalid[:, B] = 1.0 if (0 <= t < SEQ) else 0.0
    d = dict(shared)
    d["xT"] = xT
    d["kvalid"] = kvalid
    return d


def _shared_inputs(norm1_g, w_in, gmlp_v_norm_g, gmlp_ws, gmlp_b, attn_sink, attn_out_norm_g,
                   gmlp_out_norm_g, w_out, norm2_g, w_up, conv_w, conv_b, w_down, final_g):
    f = lambda a: np.ascontiguousarray(np.asarray(a, np.float32))
    col = lambda v, n: f(np.asarray(v, np.float32).reshape(n, 128).T)
    ki = np.arange(128)[:, None]
    qj = np.arange(128)[None, :]
    absrel = np.concatenate([np.abs((kb - 1) * 128 + ki - qj) for kb in range(3)], axis=1).astype(np.float32)
    mask01 = (absrel <= 128).astype(np.float32)
    cw = np.asarray(conv_w[0], np.float32)
    convw = np.concatenate([cw[k].reshape(88, 128).T for k in range(3)], axis=1)
    return {
        "w_in": f(w_in[0]), "w_out": f(w_out[0]), "w_up": f(w_up[0]), "w_down": f(w_down[0]),
        "g1": col(norm1_g[0], 16), "g2": col(norm2_g[0], 16), "gf": col(final_g, 16),
        "ga": col(attn_out_norm_g[0], 8), "gg": col(gmlp_out_norm_g[0], 8),
        "convw": f(convw), "convb": col(conv_b[0], 88),
        "sinkrep": f(np.broadcast_to(np.asarray(attn_sink[0], np.float32)[None, :], (128, 8))),
        "gvg": f(np.broadcast_to(np.asarray(gmlp_v_norm_g[0], np.float32)[None, :], (128, 1024))),
        "wsT": f(np.transpose(np.asarray(gmlp_ws[0], np.float32), (2, 0, 1)).reshape(128, 1024)),
        "brow": f(np.asarray(gmlp_b[0], np.float32).reshape(1, 1024)),
        "absrel": f(absrel), "mask01": f(mask01),
    }


_NC_CACHE = {}


def kernel(x, norm1_g, w_in, gmlp_v_norm_g, gmlp_ws, gmlp_b, attn_sink, attn_out_norm_g,
           gmlp_out_norm_g, w_out, norm2_g, w_up, conv_w, conv_b, w_down, final_g, _debug=None):
    x = np.asarray(x, np.float32)
    shared = _shared_inputs(norm1_g, w_in, gmlp_v_norm_g, gmlp_ws, gmlp_b, attn_sink, attn_out_norm_g,
                            gmlp_out_norm_g, w_out, norm2_g, w_up, conv_w, conv_b, w_down, final_g)
    in_maps = [_core_inputs(c, x, shared) for c in range(NCORE)]
    nc, dbg_out = build_nc(_debug)
    res = run_bass_kernel_spmd(nc, in_maps, core_ids=list(range(NCORE)))
    out = np.empty((2, SEQ, D), np.float32)
    for c in range(NCORE):
        bi, seg = divmod(c, NCORE // 2)
        o = np.asarray(res.results[c]["outT"]).reshape(D, OWN)
        out[bi, seg * OWN:(seg + 1) * OWN] = o.T
    if _debug is not None:
        return out, [{n: np.asarray(r["dbg_" + n]) for n in dbg_out} for r in res.results]
    return out
```

```python
import contextlib
import math
import numpy as np
import concourse.bass as bass
import concourse.mybir as mybir
from concourse.bass_utils import run_bass_kernel_spmd

F32 = mybir.dt.float32
BF16 = mybir.dt.bfloat16
AF = mybir.ActivationFunctionType
ALU = mybir.AluOpType
AX = mybir.AxisListType

D = 2048
DFF = 5632
NCORE = 8
SEQ = 16384
OWN = 4096
NB = 37
TOK = NB * 128
NST = 9
EPS = 1e-6
OWN0 = 384
OWN1 = OWN0 + OWN
XS = 16
HS = 18
NWS = 4
OPT_P1OV = True
ENGS = ("pe", "act", "dve", "pool", "sp")


class Op:
    __slots__ = ("eng", "fn", "deps", "sig", "sigval", "dkey")

    def __init__(self, eng, fn, dkey=None):
        self.eng = eng
        self.fn = fn
        self.deps = []
        self.sig = False
        self.sigval = 0
        self.dkey = dkey


class Prog:
    def __init__(self, nc, same_engine_raw=True):
        self.nc = nc
        self.ops = {e: [] for e in ENGS}
        self.last_w = {}
        self.readers = {}
        self.regions = {}
        self.same_engine_raw = same_engine_raw
        self.dkeys = {}

    def add(self, eng, fn, reads=(), writes=(), dkey=None, touch=(), join=False):
        op = Op(eng, fn, dkey)
        deps = {}
        if join:
            for r in writes:
                self.last_w[r] = op
            self.ops[eng].append(op)
            return op
        for r in reads:
            w = self.last_w.get(r)
            if w is not None:
                deps[id(w)] = (w, True)
        for r in writes:
            w = self.last_w.get(r)
            if w is not None and id(w) not in deps:
                deps[id(w)] = (w, False)
            for rd in self.readers.get(r, ()):
                if id(rd) not in deps:
                    deps[id(rd)] = (rd, False)
        for (R, tag) in touch:
            st = self.regions.setdefault(R, {"tag": None, "cur": {}, "prev": {}})
            if st["tag"] != tag:
                st["prev"] = st["cur"]
                st["cur"] = {}
                st["tag"] = tag
            for d in st["prev"].values():
                if id(d) not in deps:
                    deps[id(d)] = (d, False)
            k = ("dma", id(op)) if dkey is not None else eng
            st["cur"][k] = op
        for d, raw in deps.values():
            if d is op:
                continue
            if d.eng == eng and d.dkey is None:
                if not (raw and self.same_engine_raw):
                    continue
            op.deps.append(d)
            d.sig = True
        for r in reads:
            self.readers.setdefault(r, []).append(op)
        for r in writes:
            self.last_w[r] = op
            self.readers[r] = []
        self.ops[eng].append(op)
        if dkey is not None:
            self.dkeys.setdefault(dkey, 0)
        return op

    def emit(self, final_waits=()):
        nc = self.nc
        cnt = {e: 0 for e in ENGS}
        dcnt = {k: 0 for k in self.dkeys}
        for e in ENGS:
            for op in self.ops[e]:
                if op.dkey is not None:
                    dcnt[op.dkey] += 16
                    op.sigval = dcnt[op.dkey]
                elif op.sig:
                    cnt[e] += 1
                    op.sigval = cnt[e]
        with contextlib.ExitStack() as st:
            esem = {e: st.enter_context(nc.semaphore("s_" + e)) for e in ENGS}
            dsem = {k: st.enter_context(nc.semaphore("d_%d" % i))
                    for i, k in enumerate(self.dkeys)}
            blk = st.enter_context(nc.Block())

            def run(ename, eng):
                waited = {}
                for op in self.ops[ename]:
                    need = {}
                    for d in op.deps:
                        s = dsem[d.dkey] if d.dkey is not None else esem[d.eng]
                        k = id(s)
                        if k not in need or need[k][1] < d.sigval:
                            need[k] = (s, d.sigval)
                    for k, (s, v) in need.items():
                        if waited.get(k, 0) < v:
                            eng.wait_ge(s, v)
                            waited[k] = v
                    ins = op.fn(eng)
                    if op.dkey is not None:
                        ins.then_inc(dsem[op.dkey], 16)
                    elif op.sig:
                        ins.then_inc(esem[ename], 1)
                if ename == "sp":
                    for k in final_waits:
                        eng.wait_ge(dsem[k], dcnt[k])

            @blk.tensor
            def _(e):
                run("pe", e)

            @blk.scalar
            def _(e):
                run("act", e)

            @blk.vector
            def _(e):
                run("dve", e)

            @blk.gpsimd
            def _(e):
                run("pool", e)

            @blk.sync
            def _(e):
                run("sp", e)


def alibi_slopes(n):
    return [2.0 ** (-8.0 * (h + 1) / n) for h in range(n)]


def ffn_ranges():
    rs = []
    s = OWN0
    for j in range(NST):
        e = min(s + 510, 512 * j + 511, OWN1)
        rs.append((s, e))
        s = e
    assert s == OWN1
    return rs


def stage_panels(j):
    pl = [("win", 4), ("win", 5)]
    pl += [("win", i) for i in range(0, 4)]
    pl += [("win", 8), ("win", 9)]
    pl += [("win", i) for i in range(10, 14)]
    pl += [("win", 6), ("win", 7)]
    pl += [("wout", i) for i in range(8)]
    pl += [("wup", i) for i in range(44)]
    pl += [("wdn", i) for i in range(32)]
    return pl


def build_nc(debug=None):
    nc = bass.Bass("TRN2", target_bir_lowering=False)
    dt_in = lambda n, s, d=F32: nc.dram_tensor(n, s, d, kind="ExternalInput").ap()
    xT = dt_in("xT", [16, 128, TOK])
    w_in = dt_in("w_in", [D, 3584])
    w_out = dt_in("w_out", [D, D])
    w_up = dt_in("w_up", [D, 2 * DFF])
    w_down = dt_in("w_down", [DFF, D])
    g1_d = dt_in("g1", [128, 16])
    g2_d = dt_in("g2", [128, 16])
    gf_d = dt_in("gf", [128, 16])
    ga_d = dt_in("ga", [128, 8])
    gg_d = dt_in("gg", [128, 8])
    convw_d = dt_in("convw", [128, 3 * 88])
    convb_d = dt_in("convb", [128, 88])
    kvalid_d = dt_in("kvalid", [128, NB])
    sink_d = dt_in("sinkrep", [128, 8])
    gvg_d = dt_in("gvg", [128, 1024])
    wsT_d = dt_in("wsT", [128, 1024])
    brow_d = dt_in("brow", [1, 1024])
    absrel_d = dt_in("absrel", [128, 384])
    mask_d = dt_in("mask01", [128, 384])
    outT = nc.dram_tensor("outT", [16, 128, OWN], F32, kind="ExternalOutput").ap()

    win_s = nc.dram_tensor("win_s", [14, 128, 16, 256], BF16, kind="Internal").ap()
    wout_s = nc.dram_tensor("wout_s", [8, 128, 16, 256], BF16, kind="Internal").ap()
    wup_s = nc.dram_tensor("wup_s", [44, 128, 16, 256], BF16, kind="Internal").ap()
    wdn_s = nc.dram_tensor("wdn_s", [32, 128, 22, 128], BF16, kind="Internal").ap()
    scratch = {"win": win_s, "wout": wout_s, "wup": wup_s, "wdn": wdn_s}

    dbg_out = {}
    P = Prog(nc)
    slopes = alibi_slopes(8)
    FR = ffn_ranges()

    with contextlib.ExitStack() as st:
        def sb(name, shape, dt):
            return st.enter_context(nc.sbuf_tensor("sb_" + name, shape, dt))

        g1 = sb("g1", [128, 16], F32)
        g2 = sb("g2", [128, 16], F32)
        gf = sb("gf", [128, 16], F32)
        ga = sb("ga", [128, 8], F32)
        gg = sb("gg", [128, 8], F32)
        convw = sb("convw", [128, 3 * 88], F32)
        convb = sb("convb", [128, 88], F32)
        kvalid = sb("kvalid", [128, NB], F32)
        sinkrep = sb("sinkrep", [128, 8], F32)
        gvg = sb("gvg", [128, 1024], F32)
        wsT = sb("wsT", [128, 1024], BF16)
        bbias = sb("bbias", [2, 1024], BF16)
        bhi = bbias[0:1, :]
        blo = bbias[1:2, :]
        ones = sb("ones", [128, 128], BF16)
        dtab = sb("dtab", [128, 3 * 8 * 128], F32)
        sinkbc = sb("sinkbc", [128, 1024], F32)
        X1 = sb("X1", [128, 16, XS + 512], F32)
        xring = [sb("xr%d" % i, [128, 640], F32) for i in range(3)]
        sqA = [sb("sqa%d" % i, [128, 640], BF16) for i in range(3)]
        sqring = [sb("sq%d" % i, [128, 512], BF16) for i in range(2)]
        hreg = sb("hreg", [128, 16 * 640], BF16)
        hwin = hreg[:, :].rearrange("p (c t) -> p c t", c=16)
        h2win = hreg[:, 0:16 * (HS + 512)].rearrange("p (c t) -> p c t", c=16)
        blo0 = hreg[0:1, 4096:5120]
        h2st = sb("h2st", [128, 16, HS], BF16)
        kring = sb("kring", [128, 2, 8, 128], BF16)
        vring = sb("vring", [128, 8, 256], BF16)
        S1 = sb("S1", [128, 24576], BF16)
        qT = S1[:, 0:4096].rearrange("p (h t) -> p h t", h=8)
        uT = S1[:, 0:8192].bitcast(F32).rearrange("p (h t) -> p h t", h=8)
        anT = S1[:, 8192:16384].bitcast(F32).rearrange("p (h t) -> p h t", h=8)
        mixT = S1[:, 16384:24576].rearrange("p (h t) -> p h t", h=16)
        actT = S1[:, 0:44 * 512].rearrange("p (h t) -> p h t", h=44)
        gvall4 = S1[:, 8192:16384].bitcast(F32).rearrange("p (b c) -> p b c", b=4)
        gvall = [gvall4[:, b_, :] for b_ in range(4)]
        gv = sb("gv", [128, 1024], F32)
        vn = sb("vn", [128, 4, 1024], BF16)
        Eb = [sb("E%d" % i, [128, 512], F32) for i in range(2)]
        Pb = [sb("P%d" % i, [128, 512], BF16) for i in range(6)]
        ct = [Eb[0], Eb[1], gv[:, 0:512], gv[:, 512:1024]]
        CTK = [("E", 0), ("E", 1), ("gvh", 0), ("gvh", 1)]
        GVK2 = [("gvh", 0), ("gvh", 1)]
        rstd = sb("rstd", [128, 640], F32)
        rstd2 = sb("rstd2", [128, 512], F32)
        den = rstd[:, 0:512]
        gss = sb("gss", [128, 8], F32)
        wslot = [sb("w%d" % i, [128, 4096], BF16) for i in range(NWS)]
        ps = st.enter_context(nc.psum_tensor("ps", [128, 8, 512], F32))

        psn = [0]
        nrot = [5]

        def bank():
            b = psn[0] % nrot[0]
            psn[0] += 1
            return b

        def R_(eng, fn, reads=(), writes=(), dkey=None, touch=(), join=False):
            return P.add(eng, fn, reads=reads, writes=writes, dkey=dkey, touch=touch, join=join)

        def dma(out, in_, reads, writes, dkey, eng="sp", touch=(), join=False):
            return R_(eng, lambda e: e.dma_start(out=out, in_=in_), reads, writes, dkey, touch, join)

        def mm(out, lhsT, rhs, start, stop, reads, writes, touch=()):
            return R_("pe", lambda e: e.matmul(out, lhsT=lhsT, rhs=rhs, start=start, stop=stop),
                      reads, writes, touch=touch)

        def act(out, in_, func, reads, writes, bias=0.0, scale=1.0, touch=()):
            return R_("act", lambda e: e.activation(out=out, in_=in_, func=func, bias=bias, scale=scale),
                      reads, writes, touch=touch)

        def tt(eng, out, in0, in1, op, reads, writes, touch=()):
            return R_(eng, lambda e: e.tensor_tensor(out=out, in0=in0, in1=in1, op=op), reads, writes, touch=touch)

        def stt(eng, out, in0, scalar, in1, op0, op1, reads, writes, touch=()):
            return R_(eng, lambda e: e.scalar_tensor_tensor(out=out, in0=in0, scalar=scalar, in1=in1,
                                                            op0=op0, op1=op1), reads, writes, touch=touch)

        def cp(eng, out, in_, reads, writes, touch=()):
            return R_(eng, lambda e: e.tensor_copy(out=out, in_=in_), reads, writes, touch=touch)

        def recip(out, in_, reads, writes):
            return R_("dve", lambda e: e.reciprocal(out=out, in_=in_), reads, writes)

        def dbg(name, ap, reads, shape, touch=()):
            if debug is None or name not in debug.get("names", ()):
                return
            t = nc.dram_tensor("dbg_" + name, list(shape), ap.dtype, kind="ExternalOutput").ap()
            dbg_out[name] = t
            dma(t, ap, reads, [("dbg", name)], ("dbg", name), touch=touch)

        w_in_v = w_in.rearrange("(kc p) n -> p kc n", p=128)
        w_out_v = w_out.rearrange("(kc p) n -> p kc n", p=128)
        w_up_v = w_up.rearrange("(kc p) n -> p kc n", p=128)
        w_dn_v = w_down.rearrange("(kc p) n -> p kc n", p=128)
        w_up_v4 = w_up.rearrange("(kc p) (h n) -> p kc h n", p=128, h=2)
        NP0 = len(stage_panels(0))

        def src_f32(kind, i):
            if kind == "win":
                return w_in_v[:, :, 256 * i:256 * i + 256]
            if kind == "wout":
                return w_out_v[:, :, 256 * i:256 * i + 256]
            if kind == "wup":
                return w_up_v4[:, :, :, 128 * i:128 * i + 128]
            m, half = i // 2, i % 2
            return w_dn_v[:, 22 * half:22 * half + 22, 128 * m:128 * m + 128]

        wlist = []
        for j in range(NST):
            wlist += stage_panels(j)
        wstate = {"issued": 0, "next": 0}

        def wview(slot, kind):
            if kind == "wdn":
                return wslot[slot][:, 0:22 * 128].rearrange("p (c n) -> p c n", c=22)
            return wslot[slot][:, :].rearrange("p (c n) -> p c n", c=16)

        def wmode(idx):
            jj, loc = divmod(idx, NP0)
            if loc < 22:
                return "cast+store" if jj == 0 else "scratch"
            grp = (loc - 22) % 3
            if jj == 0:
                return "cast"
            if jj <= 3:
                if grp < jj - 1:
                    return "scratch"
                return "cast+store" if grp == jj - 1 else "cast"
            return "scratch"

        def wissue(upto):
            while wstate["issued"] < min(upto, len(wlist)):
                idx = wstate["issued"]
                kind, i = wlist[idx]
                slot = idx % NWS
                mode = wmode(idx)
                if mode == "scratch":
                    dma(wview(slot, kind), scratch[kind][i], [("scr", kind, i)], [("w", slot)], ("w", slot))
                else:
                    if kind == "wup":
                        d4 = wslot[slot][:, :].rearrange("p (c h n) -> p c h n", c=16, h=2)
                        dma(d4[:, :, 0, :], w_up_v[:, :, 128 * i:128 * i + 128], [], [("w", slot)], ("w0", slot), eng="pool")
                        dma(d4[:, :, 1, :], w_up_v[:, :, DFF + 128 * i:DFF + 128 * i + 128], [], [("w", slot)],
                            ("w0", slot), eng="pool", join=True)
                    else:
                        dma(wview(slot, kind), src_f32(kind, i), [], [("w", slot)], ("w0", slot), eng="pool")
                    if mode == "cast+store":
                        dma(scratch[kind][i], wview(slot, kind), [("w", slot)], [("scr", kind, i)], ("ws", slot))
                wstate["issued"] += 1

        def wnext(kind, i):
            idx = wstate["next"]
            assert wlist[idx] == (kind, i), (wlist[idx], kind, i)
            wissue(idx + NWS)
            wstate["next"] += 1
            slot = idx % NWS
            return wview(slot, kind), ("w", slot)

        def ld(t, d, key):
            dma(t[:], d, [], [key], ("c", key))

        ld(g1, g1_d, "g1"); ld(g2, g2_d, "g2"); ld(gf, gf_d, "gf"); ld(ga, ga_d, "ga"); ld(gg, gg_d, "gg")
        ld(convw, convw_d, "convw"); ld(convb, convb_d, "convb"); ld(kvalid, kvalid_d, "kvalid")
        ld(sinkrep, sink_d, "sinkrep"); ld(gvg, gvg_d, "gvg")
        wstg = hreg[:, 0:2048].bitcast(F32)
        dma(wstg, wsT_d, [], ["wstg"], ("c", "wsT"), touch=[("hreg", "init")])
        cp("dve", wsT[:], wstg, ["wstg"], ["wsT"], touch=[("hreg", "init")])
        for hf in range(2):
            sl = slice(512 * hf, 512 * hf + 512)
            stg = X1[0:1, 2 + hf, 0:512]
            tmp = X1[0:1, 4 + hf, 0:512]
            dma(stg, brow_d[:, sl], [], [("x1", 2 + hf)], ("c", "b%d" % hf))
            cp("dve", bhi[:, sl], stg, [("x1", 2 + hf)], [("bhi", hf)])
            cp("dve", tmp, bhi[:, sl], [("bhi", hf)], [("x1", 4 + hf)])
            tt("dve", blo0[:, sl], stg, tmp, ALU.subtract, [("x1", 2 + hf), ("x1", 4 + hf)], [("blo0", hf)],
               touch=[("hreg", "init")])
            dma(blo[:, sl], blo0[:, sl], [("blo0", hf)], [("blo", hf)], ("c", "blo%d" % hf), touch=[("hreg", "init")])
        R_("dve", lambda e: e.memset(ones[:], 1.0), [], ["ones"])
        R_("dve", lambda e: e.memset(S1[:], 0.0), [], [], touch=[("S1a0", "init"), ("S1a1", "init"), ("S1b", "init"), ("S1c", "init")])
        arel = X1[:, 0, 0:384]
        amsk = X1[:, 1, 0:384]
        dma(arel, absrel_d, [], [("x1", 0)], ("c", "absrel"))
        dma(amsk, mask_d, [], [("x1", 1)], ("c", "mask"))
        dt4 = dtab[:, :].rearrange("p (k h q) -> p k h q", k=3, h=8)
        for kb in range(3):
            for h in range(8):
                act(dt4[:, kb, h, :], arel[:, 128 * kb:128 * kb + 128], AF.Exp, [("x1", 0)], [("dt", kb, h)],
                    scale=-slopes[h])
                tt("dve", dt4[:, kb, h, :], dt4[:, kb, h, :], amsk[:, 128 * kb:128 * kb + 128], ALU.mult,
                   [("dt", kb, h), ("x1", 1)], [("dt", kb, h)])
        sb3 = sinkbc[:, :].rearrange("p (h q) -> p h q", h=8)
        for h in range(8):
            act(sb3[:, h, :], arel[:, 0:128], AF.Exp, [("x1", 0), "sinkrep"], [("sinkbc", h)],
                bias=sinkrep[:, h:h + 1], scale=0.0)
        DT_ALL = [("dt", kb, h) for kb in range(3) for h in range(8)]

        xTv = xT.rearrange("c p t -> p c t")
        outTv = outT.rearrange("c p t -> p c t")
        xr_n = [0]
        sq_n = [0]
        sqa_n = [0]
        e_n = [0]
        p_n = [0]

        def rms_finish(psb, n, nfeat, dst, rd_extra):
            act(dst, ps[:, psb, 0:n], AF.Sqrt, [("ps", psb)], rd_extra, bias=EPS, scale=1.0 / nfeat)
            recip(dst, dst, rd_extra, rd_extra)

        P1A, P1B = 5, 6
        p1q = []
        pending_tail = [None]

        def p1_pass1(jj, c):
            cc0 = 512 * jj
            xs = xr_n[0] % 3
            xr_n[0] += 1
            sq = sqa_n[0] % 3
            sqa_n[0] += 1
            dma(xring[xs][:], xTv[:, c, cc0:cc0 + 640], [], [("xr", xs)], ("xr", xs))
            act(sqA[sq][:], xring[xs][:], AF.Square, [("xr", xs)], [("sqa", sq)])
            p1q.append((c, sq))

        def p1_flush_mm(final=False):
            while len(p1q) > (0 if final else 1):
                c, sq = p1q.pop(0)
                mm(ps[:, P1A, 0:512], ones[:], sqA[sq][:, 0:512], c == 0, c == 15, ["ones", ("sqa", sq)], [("ps", P1A)])
                mm(ps[:, P1B, 0:128], ones[:], sqA[sq][:, 512:640], c == 0, c == 15, ["ones", ("sqa", sq)], [("ps", P1B)])

        def p1_finish():
            act(rstd[:, 0:512], ps[:, P1A, 0:512], AF.Sqrt, [("ps", P1A)], ["rstd"], bias=EPS, scale=1.0 / D)
            act(rstd[:, 512:640], ps[:, P1B, 0:128], AF.Sqrt, [("ps", P1B)], ["rstd"], bias=EPS, scale=1.0 / D)
            recip(rstd[:], rstd[:], ["rstd"], ["rstd"])

        def p1_pass2(jj, c):
            cc0 = 512 * jj
            xs = xr_n[0] % 3
            xr_n[0] += 1
            dma(xring[xs][:], xTv[:, c, cc0:cc0 + 640], [], [("xr", xs)], ("xr", xs))
            stt("dve", hwin[:, c, :], xring[xs][:], g1[:, c:c + 1], rstd[:],
                ALU.mult, ALU.mult, [("xr", xs), "g1", "rstd"], [("h", c)], touch=[("hreg", ("hwin", jj))])

        for j in range(NST):
            c0 = 512 * j
            mixed = [p for p in range(4) if 2 <= 4 * j + p < 36]
            s_j, e_j = FR[j]
            n_out = e_j - s_j
            S_j = c0 - s_j if j > 0 else 0
            S_n = (512 * (j + 1) - e_j) if j < NST - 1 else 0
            tg = lambda name: (name, j)
            PL = "dve" if j <= 3 else "pool"

            if j == 0 or not OPT_P1OV:
                nrot[0] = 5
                for c in range(16):
                    p1_pass1(j, c)
                    p1_flush_mm()
                p1_flush_mm(final=True)
                p1_finish()
                for c in range(16):
                    p1_pass2(j, c)
            nrot[0] = 7
            HALL = [("h", c) for c in range(16)]
            isdbg = debug is not None and debug.get("stage") == j
            if isdbg:
                dbg("hwin", hreg[:, :], HALL, [128, 16 * 640], touch=[("hreg", tg("hwin"))])

            wv, wk = wnext("win", 4)
            for kv in range(2):
                b = bank()
                for kc in range(16):
                    mm(ps[:, b, :], wv[:, kc, 128 * kv:128 * kv + 128], hwin[:, kc, 128:640], kc == 0, kc == 15,
                       [wk, ("h", kc)], [("ps", b)], touch=[("hreg", tg("hwin"))])
                for i in range(4):
                    B = 4 * j + 1 + i
                    cp("dve", kring[:, kv, B % 8, :], ps[:, b, 128 * i:128 * i + 128], [("ps", b)], [("k", B % 8, kv)])
            wv, wk = wnext("win", 5)
            for i in range(4):
                B = 4 * j + 1 + i
                b = bank()
                for kc in range(16):
                    mm(ps[:, b, 0:256], hwin[:, kc, 128 * (i + 1):128 * (i + 2)], wv[:, kc, :], kc == 0, kc == 15,
                       [wk, ("h", kc)], [("ps", b)], touch=[("hreg", tg("hwin"))])
                R_("act", lambda e, b=b, B=B: e.copy(out=vring[:, B % 8, :], in_=ps[:, b, 0:256]),
                   [("ps", b)], [("v", B % 8)])

            pending_stash = None
            pending_store = None
            if pending_tail[0] is not None:
                pending_tail[0]()
                pending_stash = pending_tail[0].stash
                pending_store = pending_tail[0].store
                pending_tail[0] = None
            for pn in range(4):
                wv, wk = wnext("win", pn)
                for mm_ in range(2):
                    h = 2 * pn + mm_
                    b = bank()
                    for kc in range(16):
                        mm(ps[:, b, :], wv[:, kc, 128 * mm_:128 * mm_ + 128], hwin[:, kc, 0:512], kc == 0, kc == 15,
                           [wk, ("h", kc)], [("ps", b)], touch=[("hreg", tg("hwin"))])
                    if h % 2 == 0:
                        cp("dve", qT[:, h, :], ps[:, b, :], [("ps", b)], [("q", h)], touch=[("S1a0", tg("q"))])
                    else:
                        R_("act", lambda e, b=b, h=h: e.copy(out=qT[:, h, :], in_=ps[:, b, :]), [("ps", b)], [("q", h)],
                           touch=[("S1a0", tg("q"))])
            if pending_store is not None:
                pending_store()
            if isdbg:
                dbg("qT", S1[:, 0:4096], [("q", h) for h in range(8)], [128, 4096], touch=[("S1a0", tg("q"))])
                dbg("kring", kring[:, :, :, :].rearrange("p a b c -> p (a b c)"),
                    [("k", s, kv) for s in range(8) for kv in range(2)], [128, 2048])
                dbg("vring", vring[:, :, :].rearrange("p a b -> p (a b)"), [("v", s) for s in range(8)], [128, 2048])

            units = [(p, kv) for p in mixed for kv in range(2)]

            def u_region(g):
                return "S1a0" if g < 4 else "S1a1"

            def attn_head(p, kv):
                B = 4 * j + p
                pbs = []
                for kb in range(3):
                    keyB = B - 1 + kb
                    bS = bank()
                    mm(ps[:, bS, :].rearrange("p (h q) -> p h q", h=4), kring[:, kv, keyB % 8, :],
                       qT[:, 4 * kv:4 * kv + 4, 128 * p:128 * p + 128],
                       True, True, [("k", keyB % 8, kv)] + [("q", h) for h in range(4 * kv, 4 * kv + 4)], [("ps", bS)],
                       touch=[("S1a0", tg("q"))])
                    eb = e_n[0] % 2
                    e_n[0] += 1
                    pb = p_n[0] % 6
                    p_n[0] += 1
                    act(Eb[eb][:], ps[:, bS, :], AF.Exp, [("ps", bS)], [("E", eb)], scale=1.0 / math.sqrt(128.0))
                    dsl = dtab[:, (kb * 8 + 4 * kv) * 128:(kb * 8 + 4 * kv + 4) * 128]
                    if keyB in (1, 2, 35, 36):
                        stt("dve", Pb[pb][:], Eb[eb][:], kvalid[:, keyB:keyB + 1], dsl, ALU.mult, ALU.mult,
                            [("E", eb), "kvalid"] + DT_ALL, [("P", pb)])
                    else:
                        tt(PL, Pb[pb][:], Eb[eb][:], dsl, ALU.mult,
                           [("E", eb)] + DT_ALL, [("P", pb)])
                    pbs.append(pb)
                return (p, kv, B, pbs)

            def attn_tail(p, kv, B, pbs):
                bO, bD = bank(), bank()
                for kb in range(3):
                    keyB = B - 1 + kb
                    pb = pbs[kb]
                    mm(ps[:, bO, :], vring[:, keyB % 8, 128 * kv:128 * kv + 128], Pb[pb][:], kb == 0, kb == 2,
                       [("v", keyB % 8), ("P", pb)], [("ps", bO)])
                    mm(ps[:, bD, :], ones[:], Pb[pb][:], kb == 0, kb == 2, ["ones", ("P", pb)], [("ps", bD)])
                tt("dve", den[:], ps[:, bD, :], sinkbc[:, 512 * kv:512 * kv + 512], ALU.add,
                   [("ps", bD)] + [("sinkbc", h) for h in range(4 * kv, 4 * kv + 4)], ["rstd"])
                recip(den[:], den[:], ["rstd"], ["rstd"])
                tt("dve", anT[:, 4 * kv:4 * kv + 4, 128 * p:128 * p + 128],
                   ps[:, bO, :].rearrange("p (h q) -> p h q", h=4),
                   den[:].rearrange("p (h q) -> p h q", h=4), ALU.mult,
                   [("ps", bO), "rstd"], [("an", p, kv)], touch=[("S1b", tg("an"))])

            ufill = [(8, 0), (8, 1), (9, 0), (9, 1)]
            uw = {}

            def u_gemm(pn, mm_):
                if pn not in uw:
                    uw[pn] = wnext("win", pn)
                wv, wk = uw[pn]
                g = 2 * (pn - 6) + mm_
                b = bank()
                for kc in range(16):
                    mm(ps[:, b, :], wv[:, kc, 128 * mm_:128 * mm_ + 128], hwin[:, kc, 0:512], kc == 0, kc == 15,
                       [wk, ("h", kc)], [("ps", b)], touch=[("hreg", tg("hwin"))])
                return g, b

            step = max(1, len(units) // 4)
            prev = None
            for ui, (p, kv) in enumerate(units):
                cur = attn_head(p, kv)
                if prev is not None:
                    attn_tail(*prev)
                prev = cur
                if ufill and ui % step == step - 1:
                    g, b = u_gemm(*ufill.pop(0))
                    cp("dve", uT[:, g, :], ps[:, b, :], [("ps", b)], [("u", g)], touch=[(u_region(g), tg("u"))])
            attn_tail(*prev)
            while ufill:
                g, b = u_gemm(*ufill.pop(0))
                cp("dve", uT[:, g, :], ps[:, b, :], [("ps", b)], [("u", g)], touch=[(u_region(g), tg("u"))])
            AN_ALL = [("an", p, kv) for (p, kv) in units]
            if isdbg:
                dbg("attn", S1[:, 8192:16384], AN_ALL, [128, 8192], touch=[("S1b", tg("an"))])

            def branch_norm(gvec, gkey, mix_off, tagname):
                bN = bank()
                for h in range(8):
                    sq = sq_n[0] % 2
                    sq_n[0] += 1
                    act(sqring[sq][:, 0:512], anT[:, h, :], AF.Square, AN_K, [("sq", sq)], touch=[("S1b", tg(tagname))])
                    mm(ps[:, bN, :], ones[:], sqring[sq][:, 0:512], h == 0, h == 7, ["ones", ("sq", sq)], [("ps", bN)])
                rms_finish(bN, 512, 1024, rstd2[:], ["rstd2"])
                for h in range(8):
                    stt("dve", mixT[:, mix_off + h, :], anT[:, h, :], gvec[:, h:h + 1], rstd2[:],
                        ALU.mult, ALU.mult, AN_K + [gkey, "rstd2"], [("mix", mix_off + h)],
                        touch=[("S1b", tg(tagname)), ("S1c", tg("mix"))])

            AN_K = AN_ALL
            branch_norm(ga, "ga", 0, "an")

            if pending_stash is not None:
                pending_stash()
            dma(X1[:, :, XS:XS + 512], xTv[:, :, c0:c0 + 512],
                [], [("x1", m) for m in range(16)], ("x1",))

            for pn in range(10, 14):
                wv, wk = wnext("win", pn)
                for p in mixed:
                    b = bank()
                    for kc in range(16):
                        mm(ps[:, b, 0:256], hwin[:, kc, 128 * p:128 * p + 128], wv[:, kc, :], kc == 0, kc == 15,
                           [wk, ("h", kc)], [("ps", b)], touch=[("hreg", tg("hwin"))])
                    act(gvall[p][:, 256 * (pn - 10):256 * (pn - 10) + 256], ps[:, b, 0:256], AF.Gelu_apprx_tanh,
                        [("ps", b)], [("gvall", p, pn - 10)], touch=[("S1b", tg("gvall"))])
            for p in mixed:
                GVK = [("gvall", p, q) for q in range(4)]
                tt("dve", gv[:], gvall[p][:], gvall[p][:], ALU.mult, GVK, GVK2, touch=[("S1b", tg("gvall"))])
                R_("dve", lambda e: e.tensor_reduce(out=gss[:], in_=gv[:].rearrange("p (g c) -> p g c", g=8),
                                                   axis=AX.X, op=ALU.add), GVK2, ["gss"])
                act(gss[:], gss[:], AF.Sqrt, ["gss"], ["gss"], bias=EPS, scale=1.0 / 128.0)
                recip(gss[:], gss[:], ["gss"], ["gss"])
                for g in range(8):
                    stt("dve", vn[:, p, 128 * g:128 * g + 128], gvall[p][:, 128 * g:128 * g + 128], gss[:, g:g + 1],
                        gvg[:, 128 * g:128 * g + 128], ALU.mult, ALU.mult, GVK + ["gss", "gvg"], [("vn", p, g)],
                        touch=[("S1b", tg("gvall"))])
            for (pn, mm_) in ((6, 0), (6, 1), (7, 0), (7, 1)):
                g, b = u_gemm(pn, mm_)
                act(uT[:, g, :], ps[:, b, :], AF.Gelu_apprx_tanh, [("ps", b)], [("u", g)], touch=[(u_region(g), tg("u"))])
            for g in range(4, 8):
                act(uT[:, g, :], uT[:, g, :], AF.Gelu_apprx_tanh, [("u", g)], [("u", g)], touch=[(u_region(g), tg("u"))])
            for g in range(8):
                b = bank()
                for p in mixed:
                    cs = slice(128 * p, 128 * p + 128)
                    mm(ps[:, b, cs], vn[:, p, 128 * g:128 * g + 128], wsT[:, 128 * g:128 * g + 128], True, False,
                       [("vn", p, g), "wsT"], [("ps", b)])
                    mm(ps[:, b, cs], ones[0:2, :], bbias[0:2, 128 * g:128 * g + 128], False, True,
                       ["ones", ("bhi", g // 4), ("blo", g // 4)], [("ps", b)])
                lo, hi = 128 * mixed[0], 128 * mixed[-1] + 128
                tt("dve", anT[:, g, lo:hi], ps[:, b, lo:hi], uT[:, g, lo:hi], ALU.mult, [("ps", b), ("u", g)], [("gm", g)],
                   touch=[("S1b", tg("gm")), (u_region(g), tg("u"))])
            AN_K = [("gm", g) for g in range(8)]
            if isdbg:
                dbg("gm", S1[:, 8192:16384], AN_K, [128, 8192], touch=[("S1b", tg("gm"))])
            branch_norm(gg, "gg", 8, "gm")
            MIX_ALL = [("mix", h) for h in range(16)]
            if isdbg:
                dbg("mixT", S1[:, 16384:24576], MIX_ALL, [128, 8192], touch=[("S1c", tg("mix"))])

            bN = 7
            pend_n = None
            for pn in range(8):
                wv, wk = wnext("wout", pn)
                for mm_ in range(2):
                    m = 2 * pn + mm_
                    b = bank()
                    for kc in range(16):
                        mm(ps[:, b, :], wv[:, kc, 128 * mm_:128 * mm_ + 128], mixT[:, kc, :], kc == 0, kc == 15,
                           [wk, ("mix", kc)], [("ps", b)], touch=[("S1c", tg("mix"))])
                    tt("dve", X1[:, m, XS:XS + 512], X1[:, m, XS:XS + 512], ps[:, b, :], ALU.add,
                       [("ps", b), ("x1", m)], [("x1", m)])
                    sq = sq_n[0] % 2
                    sq_n[0] += 1
                    act(sqring[sq][:, 0:512], X1[:, m, XS:XS + 512], AF.Square, [("x1", m)], [("sq", sq)])
                    if pend_n is not None:
                        pm, psq = pend_n
                        mm(ps[:, bN, :], ones[:], sqring[psq][:, 0:512], pm == 0, False, ["ones", ("sq", psq)], [("ps", bN)])
                    pend_n = (m, sq)
            pm, psq = pend_n
            mm(ps[:, bN, :], ones[:], sqring[psq][:, 0:512], False, True, ["ones", ("sq", psq)], [("ps", bN)])
            rms_finish(bN, 512, D, rstd2[:], ["rstd2"])
            if j == 0:
                R_("dve", lambda e: e.tensor_scalar(out=rstd2[:, 256:384], in0=rstd2[:, 256:384], scalar1=kvalid[:, 2:3],
                                                   scalar2=None, op0=ALU.mult), ["rstd2", "kvalid"], ["rstd2"])
            if j == NST - 1:
                R_("dve", lambda e: e.tensor_scalar(out=rstd2[:, 384:512], in0=rstd2[:, 384:512], scalar1=kvalid[:, 35:36],
                                                   scalar2=None, op0=ALU.mult), ["rstd2", "kvalid"], ["rstd2"])
            for m in range(16):
                stt("dve", h2win[:, m, HS:HS + 512], X1[:, m, XS:XS + 512], g2[:, m:m + 1],
                    rstd2[:], ALU.mult, ALU.mult, [("x1", m), "g2", "rstd2"], [("h2", m)], touch=[("hreg", tg("h2win"))])
            H2ALL = [("h2", m) for m in range(16)]
            if j > 0:
                cp(PL, h2win[:, :, HS - S_j - 1:HS], h2st[:, :, 0:S_j + 1], ["h2st"], ["h2s"],
                   touch=[("hreg", tg("h2win"))])
            if j < NST - 1:
                cp(PL, h2st[:, :, 0:S_n + 1], h2win[:, :, HS + 512 - S_n - 1:HS + 512], H2ALL, ["h2st"],
                   touch=[("hreg", tg("h2win"))])
            if isdbg:
                dbg("x1", X1[:, :, :].rearrange("p a b -> p (a b)"), [("x1", m) for m in range(16)], [128, 16 * (XS + 512)])

            na = n_out + 2
            ca = HS + (s_j - 1 - c0)
            for i in range(44):
                wv, wk = wnext("wup", i)
                bG, bU = bank(), bank()
                for (bb, off) in ((bG, 0), (bU, 128)):
                    for kc in range(16):
                        mm(ps[:, bb, 0:na], wv[:, kc, off:off + 128], h2win[:, kc, ca:ca + na], kc == 0, kc == 15,
                           [wk, ("h2", kc), "h2s"], [("ps", bb)], touch=[("hreg", tg("h2win"))])
                t1 = ct[(2 * i) % 4]
                t2 = ct[(2 * i + 1) % 4]
                k1 = CTK[(2 * i) % 4]
                k2 = CTK[(2 * i + 1) % 4]
                for (bb, tt_, kk, ch) in ((bG, t1, k1, i), (bU, t2, k2, 44 + i)):
                    tt_ = tt_ if (2 * i) % 4 == 0 else tt_
                    act(tt_[:, 0:n_out], ps[:, bb, 1:n_out + 1], AF.Identity, [("ps", bb), "convw", "convb"], [kk],
                        bias=convb[:, ch:ch + 1], scale=convw[:, 88 + ch:88 + ch + 1])
                    stt("dve", tt_[:, 0:n_out], ps[:, bb, 0:n_out], convw[:, ch:ch + 1], tt_[:, 0:n_out],
                        ALU.mult, ALU.add, [("ps", bb), kk, "convw"], [kk])
                    stt("dve", tt_[:, 0:n_out], ps[:, bb, 2:n_out + 2], convw[:, 176 + ch:176 + ch + 1], tt_[:, 0:n_out],
                        ALU.mult, ALU.add, [("ps", bb), kk, "convw"], [kk])
                act(t1[:, 0:n_out], t1[:, 0:n_out], AF.Silu, [k1], [k1])
                rg = "S1a0" if i < 8 else ("S1a1" if i < 16 else ("S1b" if i < 32 else "S1c"))
                tt(PL, actT[:, i, 0:n_out], t1[:, 0:n_out], t2[:, 0:n_out], ALU.mult, [k1, k2], [("act", i)],
                   touch=[(rg, tg("act"))])
            if isdbg:
                dbg("actT", S1[:, 0:44 * 512], [("act", i) for i in range(44)], [128, 44 * 512], touch=[("S1a0", tg("act")), ("S1a1", tg("act")), ("S1b", tg("act")), ("S1c", tg("act"))])

            cx = XS + (s_j - c0)
            bF = 7
            pend_f = None
            nrot[0] = 5
            for m in range(16):
                b = bank()
                for half in range(2):
                    wv, wk = wnext("wdn", 2 * m + half)
                    for kc in range(22):
                        ch = 22 * half + kc
                        rg = "S1a0" if ch < 8 else ("S1a1" if ch < 16 else ("S1b" if ch < 32 else "S1c"))
                        mm(ps[:, b, 0:n_out], wv[:, kc, :], actT[:, ch, 0:n_out], ch == 0, ch == 43,
                           [wk, ("act", ch)], [("ps", b)], touch=[(rg, tg("act"))])
                keys = [("x1", m)] + (["x1s"] if j > 0 else [])
                tt("dve", X1[:, m, cx:cx + n_out], X1[:, m, cx:cx + n_out], ps[:, b, 0:n_out], ALU.add,
                   [("ps", b)] + keys, keys)
                sq = sq_n[0] % 2
                sq_n[0] += 1
                act(sqring[sq][:, 0:n_out], X1[:, m, cx:cx + n_out], AF.Square, keys, [("sq", sq)])
                if j < NST - 1 and OPT_P1OV:
                    if m < 8:
                        p1_flush_mm()
                        p1_pass1(j + 1, 2 * m)
                        p1_pass1(j + 1, 2 * m + 1)
                    else:
                        if m == 8:
                            p1_flush_mm(final=True)
                            p1_finish()
                        p1_pass2(j + 1, 2 * (m - 8))
                        p1_pass2(j + 1, 2 * (m - 8) + 1)
                if pend_f is not None:
                    pm, psq = pend_f
                    mm(ps[:, bF, 0:n_out], ones[:], sqring[psq][:, 0:n_out], pm == 0, False, ["ones", ("sq", psq)], [("ps", bF)])
                pend_f = (m, sq)
            pm, psq = pend_f
            mm(ps[:, bF, 0:n_out], ones[:], sqring[psq][:, 0:n_out], False, True, ["ones", ("sq", psq)], [("ps", bF)])
            def make_tail(j=j, cx=cx, n_out=n_out, s_j=s_j, e_j=e_j, S_n=S_n):
                def tail():
                    rms_finish(7, n_out, D, rstd2[:, 0:n_out], ["rstd2"])
                    allk = [("x1", m) for m in range(16)] + ["x1s"]
                    for m in range(16):
                        keys = [("x1", m)] + (["x1s"] if j > 0 else [])
                        stt("dve", X1[:, m, cx:cx + n_out], X1[:, m, cx:cx + n_out], gf[:, m:m + 1],
                            rstd2[:, 0:n_out], ALU.mult, ALU.mult, keys + ["gf", "rstd2"], keys)

                def store():
                    allk = [("x1", m) for m in range(16)] + ["x1s"]
                    dma(outTv[:, :, s_j - OWN0:e_j - OWN0], X1[:, :, cx:cx + n_out], allk, ["outT"], ("out",))

                def stash():
                    allk = [("x1", m) for m in range(16)] + ["x1s"]
                    R_("sp", lambda e, a=X1[:, :, XS - S_n:XS], b_=X1[:, :, XS + 512 - S_n:XS + 512]:
                       e.dma_start(out=a, in_=b_, allow_slow_non_contiguous=True), allk, ["x1s"], ("x1st",))
                tail.stash = stash if j < NST - 1 else None
                tail.store = store
                return tail

            pending_tail[0] = make_tail()
            if j == NST - 1:
                pending_tail[0]()
                pending_tail[0].store()
                pending_tail[0] = None

        P.emit(final_waits=[("out",)] + [("dbg", n) for n in dbg_out])
    return nc, dbg_out


def _core_inputs(c, x, shared):
    bi, seg = divmod(c, NCORE // 2)
    s0 = seg * OWN
    g0 = s0 - OWN0
    xt = np.zeros((TOK, D), np.float32)
    lo, hi = max(g0, 0), min(g0 + TOK, SEQ)
    xt[lo - g0:hi - g0] = x[bi, lo:hi]
    xT = np.ascontiguousarray(xt.T).reshape(16, 128, TOK)
    kvalid = np.zeros((128, NB), np.float32)
    for B in range(NB):
        t = g0 + 128 * B
        kvalid[:, B] = 1.0 if (0 <= t < SEQ) else 0.0
    d = dict(shared)
    d["xT"] = xT
    d["kvalid"] = kvalid
    return d


def _shared_inputs(norm1_g, w_in, gmlp_v_norm_g, gmlp_ws, gmlp_b, attn_sink, attn_out_norm_g,
                   gmlp_out_norm_g, w_out, norm2_g, w_up, conv_w, conv_b, w_down, final_g):
    f = lambda a: np.ascontiguousarray(np.asarray(a, np.float32))
    col = lambda v, n: f(np.asarray(v, np.float32).reshape(n, 128).T)
    ki = np.arange(128)[:, None]
    qj = np.arange(128)[None, :]
    absrel = np.concatenate([np.abs((kb - 1) * 128 + ki - qj) for kb in range(3)], axis=1).astype(np.float32)
    mask01 = (absrel <= 128).astype(np.float32)
    cw = np.asarray(conv_w[0], np.float32)
    convw = np.concatenate([cw[k].reshape(88, 128).T for k in range(3)], axis=1)
    return {
        "w_in": f(w_in[0]), "w_out": f(w_out[0]), "w_up": f(w_up[0]), "w_down": f(w_down[0]),
        "g1": col(norm1_g[0], 16), "g2": col(norm2_g[0], 16), "gf": col(final_g, 16),
        "ga": col(attn_out_norm_g[0], 8), "gg": col(gmlp_out_norm_g[0], 8),
        "convw": f(convw), "convb": col(conv_b[0], 88),
        "sinkrep": f(np.broadcast_to(np.asarray(attn_sink[0], np.float32)[None, :], (128, 8))),
        "gvg": f(np.broadcast_to(np.asarray(gmlp_v_norm_g[0], np.float32)[None, :], (128, 1024))),
        "wsT": f(np.transpose(np.asarray(gmlp_ws[0], np.float32), (2, 0, 1)).reshape(128, 1024)),
        "brow": f(np.asarray(gmlp_b[0], np.float32).reshape(1, 1024)),
        "absrel": f(absrel), "mask01": f(mask01),
    }


_NC_CACHE = {}


def kernel(x, norm1_g, w_in, gmlp_v_norm_g, gmlp_ws, gmlp_b, attn_sink, attn_out_norm_g,
           gmlp_out_norm_g, w_out, norm2_g, w_up, conv_w, conv_b, w_down, final_g, _debug=None):
    x = np.asarray(x, np.float32)
    shared = _shared_inputs(norm1_g, w_in, gmlp_v_norm_g, gmlp_ws, gmlp_b, attn_sink, attn_out_norm_g,
                            gmlp_out_norm_g, w_out, norm2_g, w_up, conv_w, conv_b, w_down, final_g)
    in_maps = [_core_inputs(c, x, shared) for c in range(NCORE)]
    nc, dbg_out = build_nc(_debug)
    res = run_bass_kernel_spmd(nc, in_maps, core_ids=list(range(NCORE)))
    out = np.empty((2, SEQ, D), np.float32)
    for c in range(NCORE):
        bi, seg = divmod(c, NCORE // 2)
        o = np.asarray(res.results[c]["outT"]).reshape(D, OWN)
        out[bi, seg * OWN:(seg + 1) * OWN] = o.T
    if _debug is not None:
        return out, [{n: np.asarray(r["dbg_" + n]) for n in dbg_out} for r in res.results]
    return out
```

```python
import contextlib
import math
import numpy as np
import concourse.bass as bass
import concourse.mybir as mybir
from concourse.bass_utils import run_bass_kernel_spmd

F32 = mybir.dt.float32
BF16 = mybir.dt.bfloat16
AF = mybir.ActivationFunctionType
ALU = mybir.AluOpType
AX = mybir.AxisListType

D = 2048
DFF = 5632
NCORE = 8
SEQ = 16384
OWN = 4096
NB = 37
TOK = NB * 128
NST = 9
EPS = 1e-6
OWN0 = 384
OWN1 = OWN0 + OWN
XS = 16
HS = 18
NWS = 4
OPT_P1OV = True
ENGS = ("pe", "act", "dve", "pool", "sp")


class Op:
    __slots__ = ("eng", "fn", "deps", "sig", "sigval", "dkey")

    def __init__(self, eng, fn, dkey=None):
        self.eng = eng
        self.fn = fn
        self.deps = []
        self.sig = False
        self.sigval = 0
        self.dkey = dkey


class Prog:
    def __init__(self, nc, same_engine_raw=True):
        self.nc = nc
        self.ops = {e: [] for e in ENGS}
        self.last_w = {}
        self.readers = {}
        self.regions = {}
        self.same_engine_raw = same_engine_raw
        self.dkeys = {}

    def add(self, eng, fn, reads=(), writes=(), dkey=None, touch=(), join=False):
        op = Op(eng, fn, dkey)
        deps = {}
        if join:
            for r in writes:
                self.last_w[r] = op
            self.ops[eng].append(op)
            return op
        for r in reads:
            w = self.last_w.get(r)
            if w is not None:
                deps[id(w)] = (w, True)
        for r in writes:
            w = self.last_w.get(r)
            if w is not None and id(w) not in deps:
                deps[id(w)] = (w, False)
            for rd in self.readers.get(r, ()):
                if id(rd) not in deps:
                    deps[id(rd)] = (rd, False)
        for (R, tag) in touch:
            st = self.regions.setdefault(R, {"tag": None, "cur": {}, "prev": {}})
            if st["tag"] != tag:
                st["prev"] = st["cur"]
                st["cur"] = {}
                st["tag"] = tag
            for d in st["prev"].values():
                if id(d) not in deps:
                    deps[id(d)] = (d, False)
            k = ("dma", id(op)) if dkey is not None else eng
            st["cur"][k] = op
        for d, raw in deps.values():
            if d is op:
                continue
            if d.eng == eng and d.dkey is None:
                if not (raw and self.same_engine_raw):
                    continue
            op.deps.append(d)
            d.sig = True
        for r in reads:
            self.readers.setdefault(r, []).append(op)
        for r in writes:
            self.last_w[r] = op
            self.readers[r] = []
        self.ops[eng].append(op)
        if dkey is not None:
            self.dkeys.setdefault(dkey, 0)
        return op

    def emit(self, final_waits=()):
        nc = self.nc
        cnt = {e: 0 for e in ENGS}
        dcnt = {k: 0 for k in self.dkeys}
        for e in ENGS:
            for op in self.ops[e]:
                if op.dkey is not None:
                    dcnt[op.dkey] += 16
                    op.sigval = dcnt[op.dkey]
                elif op.sig:
                    cnt[e] += 1
                    op.sigval = cnt[e]
        with contextlib.ExitStack() as st:
            esem = {e: st.enter_context(nc.semaphore("s_" + e)) for e in ENGS}
            dsem = {k: st.enter_context(nc.semaphore("d_%d" % i))
                    for i, k in enumerate(self.dkeys)}
            blk = st.enter_context(nc.Block())

            def run(ename, eng):
                waited = {}
                for op in self.ops[ename]:
                    need = {}
                    for d in op.deps:
                        s = dsem[d.dkey] if d.dkey is not None else esem[d.eng]
                        k = id(s)
                        if k not in need or need[k][1] < d.sigval:
                            need[k] = (s, d.sigval)
                    for k, (s, v) in need.items():
                        if waited.get(k, 0) < v:
                            eng.wait_ge(s, v)
                            waited[k] = v
                    ins = op.fn(eng)
                    if op.dkey is not None:
                        ins.then_inc(dsem[op.dkey], 16)
                    elif op.sig:
                        ins.then_inc(esem[ename], 1)
                if ename == "sp":
                    for k in final_waits:
                        eng.wait_ge(dsem[k], dcnt[k])

            @blk.tensor
            def _(e):
                run("pe", e)

            @blk.scalar
            def _(e):
                run("act", e)

            @blk.vector
            def _(e):
                run("dve", e)

            @blk.gpsimd
            def _(e):
                run("pool", e)

            @blk.sync
            def _(e):
                run("sp", e)


def alibi_slopes(n):
    return [2.0 ** (-8.0 * (h + 1) / n) for h in range(n)]


def ffn_ranges():
    rs = []
    s = OWN0
    for j in range(NST):
        e = min(s + 510, 512 * j + 511, OWN1)
        rs.append((s, e))
        s = e
    assert s == OWN1
    return rs


def stage_panels(j):
    pl = [("win", 4), ("win", 5)]
    pl += [("win", i) for i in range(0, 4)]
    pl += [("win", 8), ("win", 9)]
    pl += [("win", i) for i in range(10, 14)]
    pl += [("win", 6), ("win", 7)]
    pl += [("wout", i) for i in range(8)]
    pl += [("wup", i) for i in range(44)]
    pl += [("wdn", i) for i in range(32)]
    return pl


def build_nc(debug=None):
    nc = bass.Bass("TRN2", target_bir_lowering=False)
    dt_in = lambda n, s, d=F32: nc.dram_tensor(n, s, d, kind="ExternalInput").ap()
    xT = dt_in("xT", [16, 128, TOK])
    w_in = dt_in("w_in", [D, 3584])
    w_out = dt_in("w_out", [D, D])
    w_up = dt_in("w_up", [D, 2 * DFF])
    w_down = dt_in("w_down", [DFF, D])
    g1_d = dt_in("g1", [128, 16])
    g2_d = dt_in("g2", [128, 16])
    gf_d = dt_in("gf", [128, 16])
    ga_d = dt_in("ga", [128, 8])
    gg_d = dt_in("gg", [128, 8])
    convw_d = dt_in("convw", [128, 3 * 88])
    convb_d = dt_in("convb", [128, 88])
    kvalid_d = dt_in("kvalid", [128, NB])
    sink_d = dt_in("sinkrep", [128, 8])
    gvg_d = dt_in("gvg", [128, 1024])
    wsT_d = dt_in("wsT", [128, 1024])
    brow_d = dt_in("brow", [1, 1024])
    absrel_d = dt_in("absrel", [128, 384])
    mask_d = dt_in("mask01", [128, 384])
    outT = nc.dram_tensor("outT", [16, 128, OWN], F32, kind="ExternalOutput").ap()

    win_s = nc.dram_tensor("win_s", [14, 128, 16, 256], BF16, kind="Internal").ap()
    wout_s = nc.dram_tensor("wout_s", [8, 128, 16, 256], BF16, kind="Internal").ap()
    wup_s = nc.dram_tensor("wup_s", [44, 128, 16, 256], BF16, kind="Internal").ap()
    wdn_s = nc.dram_tensor("wdn_s", [32, 128, 22, 128], BF16, kind="Internal").ap()
    scratch = {"win": win_s, "wout": wout_s, "wup": wup_s, "wdn": wdn_s}

    dbg_out = {}
    P = Prog(nc)
    slopes = alibi_slopes(8)
    FR = ffn_ranges()

    with contextlib.ExitStack() as st:
        def sb(name, shape, dt):
            return st.enter_context(nc.sbuf_tensor("sb_" + name, shape, dt))

        g1 = sb("g1", [128, 16], F32)
        g2 = sb("g2", [128, 16], F32)
        gf = sb("gf", [128, 16], F32)
        ga = sb("ga", [128, 8], F32)
        gg = sb("gg", [128, 8], F32)
        convw = sb("convw", [128, 3 * 88], F32)
        convb = sb("convb", [128, 88], F32)
        kvalid = sb("kvalid", [128, NB], F32)
        sinkrep = sb("sinkrep", [128, 8], F32)
        gvg = sb("gvg", [128, 1024], F32)
        wsT = sb("wsT", [128, 1024], BF16)
        bbias = sb("bbias", [2, 1024], BF16)
        bhi = bbias[0:1, :]
        blo = bbias[1:2, :]
        ones = sb("ones", [128, 128], BF16)
        dtab = sb("dtab", [128, 3 * 8 * 128], F32)
        sinkbc = sb("sinkbc", [128, 1024], F32)
        X1 = sb("X1", [128, 16, XS + 512], F32)
        xring = [sb("xr%d" % i, [128, 640], F32) for i in range(3)]
        sqA = [sb("sqa%d" % i, [128, 640], BF16) for i in range(3)]
        sqring = [sb("sq%d" % i, [128, 512], BF16) for i in range(2)]
        hreg = sb("hreg", [128, 16 * 640], BF16)
        hwin = hreg[:, :].rearrange("p (c t) -> p c t", c=16)
        h2win = hreg[:, 0:16 * (HS + 512)].rearrange("p (c t) -> p c t", c=16)
        blo0 = hreg[0:1, 4096:5120]
        h2st = sb("h2st", [128, 16, HS], BF16)
        kring = sb("kring", [128, 2, 8, 128], BF16)
        vring = sb("vring", [128, 8, 256], BF16)
        S1 = sb("S1", [128, 24576], BF16)
        qT = S1[:, 0:4096].rearrange("p (h t) -> p h t", h=8)
        uT = S1[:, 0:8192].bitcast(F32).rearrange("p (h t) -> p h t", h=8)
        anT = S1[:, 8192:16384].bitcast(F32).rearrange("p (h t) -> p h t", h=8)
        mixT = S1[:, 16384:24576].rearrange("p (h t) -> p h t", h=16)
        actT = S1[:, 0:44 * 512].rearrange("p (h t) -> p h t", h=44)
        gvall4 = S1[:, 8192:16384].bitcast(F32).rearrange("p (b c) -> p b c", b=4)
        gvall = [gvall4[:, b_, :] for b_ in range(4)]
        gv = sb("gv", [128, 1024], F32)
        vn = sb("vn", [128, 4, 1024], BF16)
        Eb = [sb("E%d" % i, [128, 512], F32) for i in range(2)]
        Pb = [sb("P%d" % i, [128, 512], BF16) for i in range(6)]
        ct = [Eb[0], Eb[1], gv[:, 0:512], gv[:, 512:1024]]
        CTK = [("E", 0), ("E", 1), ("gvh", 0), ("gvh", 1)]
        GVK2 = [("gvh", 0), ("gvh", 1)]
        rstd = sb("rstd", [128, 640], F32)
        rstd2 = sb("rstd2", [128, 512], F32)
        den = rstd[:, 0:512]
        gss = sb("gss", [128, 8], F32)
        wslot = [sb("w%d" % i, [128, 4096], BF16) for i in range(NWS)]
        ps = st.enter_context(nc.psum_tensor("ps", [128, 8, 512], F32))

        psn = [0]
        nrot = [5]

        def bank():
            b = psn[0] % nrot[0]
            psn[0] += 1
            return b

        def R_(eng, fn, reads=(), writes=(), dkey=None, touch=(), join=False):
            return P.add(eng, fn, reads=reads, writes=writes, dkey=dkey, touch=touch, join=join)

        def dma(out, in_, reads, writes, dkey, eng="sp", touch=(), join=False):
            return R_(eng, lambda e: e.dma_start(out=out, in_=in_), reads, writes, dkey, touch, join)

        def mm(out, lhsT, rhs, start, stop, reads, writes, touch=()):
            return R_("pe", lambda e: e.matmul(out, lhsT=lhsT, rhs=rhs, start=start, stop=stop),
                      reads, writes, touch=touch)

        def act(out, in_, func, reads, writes, bias=0.0, scale=1.0, touch=()):
            return R_("act", lambda e: e.activation(out=out, in_=in_, func=func, bias=bias, scale=scale),
                      reads, writes, touch=touch)

        def tt(eng, out, in0, in1, op, reads, writes, touch=()):
            return R_(eng, lambda e: e.tensor_tensor(out=out, in0=in0, in1=in1, op=op), reads, writes, touch=touch)

        def stt(eng, out, in0, scalar, in1, op0, op1, reads, writes, touch=()):
            return R_(eng, lambda e: e.scalar_tensor_tensor(out=out, in0=in0, scalar=scalar, in1=in1,
                                                            op0=op0, op1=op1), reads, writes, touch=touch)

        def cp(eng, out, in_, reads, writes, touch=()):
            return R_(eng, lambda e: e.tensor_copy(out=out, in_=in_), reads, writes, touch=touch)

        def recip(out, in_, reads, writes):
            return R_("dve", lambda e: e.reciprocal(out=out, in_=in_), reads, writes)

        def dbg(name, ap, reads, shape, touch=()):
            if debug is None or name not in debug.get("names", ()):
                return
            t = nc.dram_tensor("dbg_" + name, list(shape), ap.dtype, kind="ExternalOutput").ap()
            dbg_out[name] = t
            dma(t, ap, reads, [("dbg", name)], ("dbg", name), touch=touch)

        w_in_v = w_in.rearrange("(kc p) n -> p kc n", p=128)
        w_out_v = w_out.rearrange("(kc p) n -> p kc n", p=128)
        w_up_v = w_up.rearrange("(kc p) n -> p kc n", p=128)
        w_dn_v = w_down.rearrange("(kc p) n -> p kc n", p=128)
        w_up_v4 = w_up.rearrange("(kc p) (h n) -> p kc h n", p=128, h=2)
        NP0 = len(stage_panels(0))

        def src_f32(kind, i):
            if kind == "win":
                return w_in_v[:, :, 256 * i:256 * i + 256]
            if kind == "wout":
                return w_out_v[:, :, 256 * i:256 * i + 256]
            if kind == "wup":
                return w_up_v4[:, :, :, 128 * i:128 * i + 128]
            m, half = i // 2, i % 2
            return w_dn_v[:, 22 * half:22 * half + 22, 128 * m:128 * m + 128]

        wlist = []
        for j in range(NST):
            wlist += stage_panels(j)
        wstate = {"issued": 0, "next": 0}

        def wview(slot, kind):
            if kind == "wdn":
                return wslot[slot][:, 0:22 * 128].rearrange("p (c n) -> p c n", c=22)
            return wslot[slot][:, :].rearrange("p (c n) -> p c n", c=16)

        def wmode(idx):
            jj, loc = divmod(idx, NP0)
            if loc < 22:
                return "cast+store" if jj == 0 else "scratch"
            grp = (loc - 22) % 2
            if jj == 0:
                return "cast"
            if jj <= 2:
                if grp < jj - 1:
                    return "scratch"
                return "cast+store" if grp == jj - 1 else "cast"
            return "scratch"

        def wissue(upto):
            while wstate["issued"] < min(upto, len(wlist)):
                idx = wstate["issued"]
                kind, i = wlist[idx]
                slot = idx % NWS
                mode = wmode(idx)
                if mode == "scratch":
                    dma(wview(slot, kind), scratch[kind][i], [("scr", kind, i)], [("w", slot)], ("w", slot))
                else:
                    if kind == "wup":
                        d4 = wslot[slot][:, :].rearrange("p (c h n) -> p c h n", c=16, h=2)
                        dma(d4[:, :, 0, :], w_up_v[:, :, 128 * i:128 * i + 128], [], [("w", slot)], ("w0", slot), eng="pool")
                        dma(d4[:, :, 1, :], w_up_v[:, :, DFF + 128 * i:DFF + 128 * i + 128], [], [("w", slot)],
                            ("w0", slot), eng="pool", join=True)
                    else:
                        dma(wview(slot, kind), src_f32(kind, i), [], [("w", slot)], ("w0", slot), eng="pool")
                    if mode == "cast+store":
                        dma(scratch[kind][i], wview(slot, kind), [("w", slot)], [("scr", kind, i)], ("ws", slot))
                wstate["issued"] += 1

        def wnext(kind, i):
            idx = wstate["next"]
            assert wlist[idx] == (kind, i), (wlist[idx], kind, i)
            wissue(idx + NWS)
            wstate["next"] += 1
            slot = idx % NWS
            return wview(slot, kind), ("w", slot)

        def ld(t, d, key):
            dma(t[:], d, [], [key], ("c", key))

        ld(g1, g1_d, "g1"); ld(g2, g2_d, "g2"); ld(gf, gf_d, "gf"); ld(ga, ga_d, "ga"); ld(gg, gg_d, "gg")
        ld(convw, convw_d, "convw"); ld(convb, convb_d, "convb"); ld(kvalid, kvalid_d, "kvalid")
        ld(sinkrep, sink_d, "sinkrep"); ld(gvg, gvg_d, "gvg")
        wstg = hreg[:, 0:2048].bitcast(F32)
        dma(wstg, wsT_d, [], ["wstg"], ("c", "wsT"), touch=[("hreg", "init")])
        cp("dve", wsT[:], wstg, ["wstg"], ["wsT"], touch=[("hreg", "init")])
        for hf in range(2):
            sl = slice(512 * hf, 512 * hf + 512)
            stg = X1[0:1, 2 + hf, 0:512]
            tmp = X1[0:1, 4 + hf, 0:512]
            dma(stg, brow_d[:, sl], [], [("x1", 2 + hf)], ("c", "b%d" % hf))
            cp("dve", bhi[:, sl], stg, [("x1", 2 + hf)], [("bhi", hf)])
            cp("dve", tmp, bhi[:, sl], [("bhi", hf)], [("x1", 4 + hf)])
            tt("dve", blo0[:, sl], stg, tmp, ALU.subtract, [("x1", 2 + hf), ("x1", 4 + hf)], [("blo0", hf)],
               touch=[("hreg", "init")])
            dma(blo[:, sl], blo0[:, sl], [("blo0", hf)], [("blo", hf)], ("c", "blo%d" % hf), touch=[("hreg", "init")])
        R_("dve", lambda e: e.memset(ones[:], 1.0), [], ["ones"])
        R_("dve", lambda e: e.memset(S1[:], 0.0), [], [], touch=[("S1a0", "init"), ("S1a1", "init"), ("S1b", "init"), ("S1c", "init")])
        arel = X1[:, 0, 0:384]
        amsk = X1[:, 1, 0:384]
        dma(arel, absrel_d, [], [("x1", 0)], ("c", "absrel"))
        dma(amsk, mask_d, [], [("x1", 1)], ("c", "mask"))
        dt4 = dtab[:, :].rearrange("p (k h q) -> p k h q", k=3, h=8)
        for kb in range(3):
            for h in range(8):
                act(dt4[:, kb, h, :], arel[:, 128 * kb:128 * kb + 128], AF.Exp, [("x1", 0)], [("dt", kb, h)],
                    scale=-slopes[h])
                tt("dve", dt4[:, kb, h, :], dt4[:, kb, h, :], amsk[:, 128 * kb:128 * kb + 128], ALU.mult,
                   [("dt", kb, h), ("x1", 1)], [("dt", kb, h)])
        sb3 = sinkbc[:, :].rearrange("p (h q) -> p h q", h=8)
        for h in range(8):
            act(sb3[:, h, :], arel[:, 0:128], AF.Exp, [("x1", 0), "sinkrep"], [("sinkbc", h)],
                bias=sinkrep[:, h:h + 1], scale=0.0)
        DT_ALL = [("dt", kb, h) for kb in range(3) for h in range(8)]

        xTv = xT.rearrange("c p t -> p c t")
        outTv = outT.rearrange("c p t -> p c t")
        xr_n = [0]
        sq_n = [0]
        sqa_n = [0]
        e_n = [0]
        p_n = [0]

        def rms_finish(psb, n, nfeat, dst, rd_extra):
            act(dst, ps[:, psb, 0:n], AF.Sqrt, [("ps", psb)], rd_extra, bias=EPS, scale=1.0 / nfeat)
            recip(dst, dst, rd_extra, rd_extra)

        P1A, P1B = 5, 6
        p1q = []
        pending_tail = [None]

        def p1_pass1(jj, c):
            cc0 = 512 * jj
            xs = xr_n[0] % 3
            xr_n[0] += 1
            sq = sqa_n[0] % 3
            sqa_n[0] += 1
            dma(xring[xs][:], xTv[:, c, cc0:cc0 + 640], [], [("xr", xs)], ("xr", xs))
            act(sqA[sq][:], xring[xs][:], AF.Square, [("xr", xs)], [("sqa", sq)])
            p1q.append((c, sq))

        def p1_flush_mm(final=False):
            while len(p1q) > (0 if final else 1):
                c, sq = p1q.pop(0)
                mm(ps[:, P1A, 0:512], ones[:], sqA[sq][:, 0:512], c == 0, c == 15, ["ones", ("sqa", sq)], [("ps", P1A)])
                mm(ps[:, P1B, 0:128], ones[:], sqA[sq][:, 512:640], c == 0, c == 15, ["ones", ("sqa", sq)], [("ps", P1B)])

        def p1_finish():
            act(rstd[:, 0:512], ps[:, P1A, 0:512], AF.Sqrt, [("ps", P1A)], ["rstd"], bias=EPS, scale=1.0 / D)
            act(rstd[:, 512:640], ps[:, P1B, 0:128], AF.Sqrt, [("ps", P1B)], ["rstd"], bias=EPS, scale=1.0 / D)
            recip(rstd[:], rstd[:], ["rstd"], ["rstd"])

        def p1_pass2(jj, c):
            cc0 = 512 * jj
            xs = xr_n[0] % 3
            xr_n[0] += 1
            dma(xring[xs][:], xTv[:, c, cc0:cc0 + 640], [], [("xr", xs)], ("xr", xs))
            stt("dve", hwin[:, c, :], xring[xs][:], g1[:, c:c + 1], rstd[:],
                ALU.mult, ALU.mult, [("xr", xs), "g1", "rstd"], [("h", c)], touch=[("hreg", ("hwin", jj))])

        for j in range(NST):
            c0 = 512 * j
            mixed = [p for p in range(4) if 2 <= 4 * j + p < 36]
            s_j, e_j = FR[j]
            n_out = e_j - s_j
            S_j = c0 - s_j if j > 0 else 0
            S_n = (512 * (j + 1) - e_j) if j < NST - 1 else 0
            tg = lambda name: (name, j)
            PL = "dve" if j <= 2 else "pool"

            if j == 0 or not OPT_P1OV:
                nrot[0] = 5
                for c in range(16):
                    p1_pass1(j, c)
                    p1_flush_mm()
                p1_flush_mm(final=True)
                p1_finish()
                for c in range(16):
                    p1_pass2(j, c)
            nrot[0] = 7
            HALL = [("h", c) for c in range(16)]
            isdbg = debug is not None and debug.get("stage") == j
            if isdbg:
                dbg("hwin", hreg[:, :], HALL, [128, 16 * 640], touch=[("hreg", tg("hwin"))])

            wv, wk = wnext("win", 4)
            for kv in range(2):
                b = bank()
                for kc in range(16):
                    mm(ps[:, b, :], wv[:, kc, 128 * kv:128 * kv + 128], hwin[:, kc, 128:640], kc == 0, kc == 15,
                       [wk, ("h", kc)], [("ps", b)], touch=[("hreg", tg("hwin"))])
                for i in range(4):
                    B = 4 * j + 1 + i
                    cp("dve", kring[:, kv, B % 8, :], ps[:, b, 128 * i:128 * i + 128], [("ps", b)], [("k", B % 8, kv)])
            wv, wk = wnext("win", 5)
            for i in range(4):
                B = 4 * j + 1 + i
                b = bank()
                for kc in range(16):
                    mm(ps[:, b, 0:256], hwin[:, kc, 128 * (i + 1):128 * (i + 2)], wv[:, kc, :], kc == 0, kc == 15,
                       [wk, ("h", kc)], [("ps", b)], touch=[("hreg", tg("hwin"))])
                R_("act", lambda e, b=b, B=B: e.copy(out=vring[:, B % 8, :], in_=ps[:, b, 0:256]),
                   [("ps", b)], [("v", B % 8)])

            pending_stash = None
            pending_store = None
            if pending_tail[0] is not None:
                pending_tail[0]()
                pending_stash = pending_tail[0].stash
                pending_store = pending_tail[0].store
                pending_tail[0] = None
            for pn in range(4):
                wv, wk = wnext("win", pn)
                for mm_ in range(2):
                    h = 2 * pn + mm_
                    b = bank()
                    for kc in range(16):
                        mm(ps[:, b, :], wv[:, kc, 128 * mm_:128 * mm_ + 128], hwin[:, kc, 0:512], kc == 0, kc == 15,
                           [wk, ("h", kc)], [("ps", b)], touch=[("hreg", tg("hwin"))])
                    if h % 2 == 0:
                        cp("dve", qT[:, h, :], ps[:, b, :], [("ps", b)], [("q", h)], touch=[("S1a0", tg("q"))])
                    else:
                        R_("act", lambda e, b=b, h=h: e.copy(out=qT[:, h, :], in_=ps[:, b, :]), [("ps", b)], [("q", h)],
                           touch=[("S1a0", tg("q"))])
            if pending_store is not None:
                pending_store()
            if isdbg:
                dbg("qT", S1[:, 0:4096], [("q", h) for h in range(8)], [128, 4096], touch=[("S1a0", tg("q"))])
                dbg("kring", kring[:, :, :, :].rearrange("p a b c -> p (a b c)"),
                    [("k", s, kv) for s in range(8) for kv in range(2)], [128, 2048])
                dbg("vring", vring[:, :, :].rearrange("p a b -> p (a b)"), [("v", s) for s in range(8)], [128, 2048])

            units = [(p, kv) for p in mixed for kv in range(2)]

            def u_region(g):
                return "S1a0" if g < 4 else "S1a1"

            def attn_head(p, kv):
                B = 4 * j + p
                pbs = []
                for kb in range(3):
                    keyB = B - 1 + kb
                    bS = bank()
                    mm(ps[:, bS, :].rearrange("p (h q) -> p h q", h=4), kring[:, kv, keyB % 8, :],
                       qT[:, 4 * kv:4 * kv + 4, 128 * p:128 * p + 128],
                       True, True, [("k", keyB % 8, kv)] + [("q", h) for h in range(4 * kv, 4 * kv + 4)], [("ps", bS)],
                       touch=[("S1a0", tg("q"))])
                    eb = e_n[0] % 2
                    e_n[0] += 1
                    pb = p_n[0] % 6
                    p_n[0] += 1
                    act(Eb[eb][:], ps[:, bS, :], AF.Exp, [("ps", bS)], [("E", eb)], scale=1.0 / math.sqrt(128.0))
                    dsl = dtab[:, (kb * 8 + 4 * kv) * 128:(kb * 8 + 4 * kv + 4) * 128]
                    if keyB in (1, 2, 35, 36):
                        stt("dve", Pb[pb][:], Eb[eb][:], kvalid[:, keyB:keyB + 1], dsl, ALU.mult, ALU.mult,
                            [("E", eb), "kvalid"] + DT_ALL, [("P", pb)])
                    else:
                        tt(PL, Pb[pb][:], Eb[eb][:], dsl, ALU.mult,
                           [("E", eb)] + DT_ALL, [("P", pb)])
                    pbs.append(pb)
                return (p, kv, B, pbs)

            def attn_tail(p, kv, B, pbs):
                bO, bD = bank(), bank()
                for kb in range(3):
                    keyB = B - 1 + kb
                    pb = pbs[kb]
                    mm(ps[:, bO, :], vring[:, keyB % 8, 128 * kv:128 * kv + 128], Pb[pb][:], kb == 0, kb == 2,
                       [("v", keyB % 8), ("P", pb)], [("ps", bO)])
                    mm(ps[:, bD, :], ones[:], Pb[pb][:], kb == 0, kb == 2, ["ones", ("P", pb)], [("ps", bD)])
                tt("dve", den[:], ps[:, bD, :], sinkbc[:, 512 * kv:512 * kv + 512], ALU.add,
                   [("ps", bD)] + [("sinkbc", h) for h in range(4 * kv, 4 * kv + 4)], ["rstd"])
                recip(den[:], den[:], ["rstd"], ["rstd"])
                tt("dve", anT[:, 4 * kv:4 * kv + 4, 128 * p:128 * p + 128],
                   ps[:, bO, :].rearrange("p (h q) -> p h q", h=4),
                   den[:].rearrange("p (h q) -> p h q", h=4), ALU.mult,
                   [("ps", bO), "rstd"], [("an", p, kv)], touch=[("S1b", tg("an"))])

            ufill = [(8, 0), (8, 1), (9, 0), (9, 1)]
            uw = {}

            def u_gemm(pn, mm_):
                if pn not in uw:
                    uw[pn] = wnext("win", pn)
                wv, wk = uw[pn]
                g = 2 * (pn - 6) + mm_
                b = bank()
                for kc in range(16):
                    mm(ps[:, b, :], wv[:, kc, 128 * mm_:128 * mm_ + 128], hwin[:, kc, 0:512], kc == 0, kc == 15,
                       [wk, ("h", kc)], [("ps", b)], touch=[("hreg", tg("hwin"))])
                return g, b

            step = max(1, len(units) // 4)
            prev = None
            for ui, (p, kv) in enumerate(units):
                cur = attn_head(p, kv)
                if prev is not None:
                    attn_tail(*prev)
                prev = cur
                if ufill and ui % step == step - 1:
                    g, b = u_gemm(*ufill.pop(0))
                    cp("dve", uT[:, g, :], ps[:, b, :], [("ps", b)], [("u", g)], touch=[(u_region(g), tg("u"))])
            attn_tail(*prev)
            while ufill:
                g, b = u_gemm(*ufill.pop(0))
                cp("dve", uT[:, g, :], ps[:, b, :], [("ps", b)], [("u", g)], touch=[(u_region(g), tg("u"))])
            AN_ALL = [("an", p, kv) for (p, kv) in units]
            if isdbg:
                dbg("attn", S1[:, 8192:16384], AN_ALL, [128, 8192], touch=[("S1b", tg("an"))])

            def branch_norm(gvec, gkey, mix_off, tagname):
                bN = bank()
                for h in range(8):
                    sq = sq_n[0] % 2
                    sq_n[0] += 1
                    act(sqring[sq][:, 0:512], anT[:, h, :], AF.Square, AN_K, [("sq", sq)], touch=[("S1b", tg(tagname))])
                    mm(ps[:, bN, :], ones[:], sqring[sq][:, 0:512], h == 0, h == 7, ["ones", ("sq", sq)], [("ps", bN)])
                rms_finish(bN, 512, 1024, rstd2[:], ["rstd2"])
                for h in range(8):
                    stt("dve", mixT[:, mix_off + h, :], anT[:, h, :], gvec[:, h:h + 1], rstd2[:],
                        ALU.mult, ALU.mult, AN_K + [gkey, "rstd2"], [("mix", mix_off + h)],
                        touch=[("S1b", tg(tagname)), ("S1c", tg("mix"))])

            AN_K = AN_ALL
            branch_norm(ga, "ga", 0, "an")

            if pending_stash is not None:
                pending_stash()
            dma(X1[:, :, XS:XS + 512], xTv[:, :, c0:c0 + 512],
                [], [("x1", m) for m in range(16)], ("x1",))

            for pn in range(10, 14):
                wv, wk = wnext("win", pn)
                for p in mixed:
                    b = bank()
                    for kc in range(16):
                        mm(ps[:, b, 0:256], hwin[:, kc, 128 * p:128 * p + 128], wv[:, kc, :], kc == 0, kc == 15,
                           [wk, ("h", kc)], [("ps", b)], touch=[("hreg", tg("hwin"))])
                    act(gvall[p][:, 256 * (pn - 10):256 * (pn - 10) + 256], ps[:, b, 0:256], AF.Gelu_apprx_tanh,
                        [("ps", b)], [("gvall", p, pn - 10)], touch=[("S1b", tg("gvall"))])
            for p in mixed:
                GVK = [("gvall", p, q) for q in range(4)]
                tt("dve", gv[:], gvall[p][:], gvall[p][:], ALU.mult, GVK, GVK2, touch=[("S1b", tg("gvall"))])
                R_("dve", lambda e: e.tensor_reduce(out=gss[:], in_=gv[:].rearrange("p (g c) -> p g c", g=8),
                                                   axis=AX.X, op=ALU.add), GVK2, ["gss"])
                act(gss[:], gss[:], AF.Sqrt, ["gss"], ["gss"], bias=EPS, scale=1.0 / 128.0)
                recip(gss[:], gss[:], ["gss"], ["gss"])
                for g in range(8):
                    stt("dve", vn[:, p, 128 * g:128 * g + 128], gvall[p][:, 128 * g:128 * g + 128], gss[:, g:g + 1],
                        gvg[:, 128 * g:128 * g + 128], ALU.mult, ALU.mult, GVK + ["gss", "gvg"], [("vn", p, g)],
                        touch=[("S1b", tg("gvall"))])
            for (pn, mm_) in ((6, 0), (6, 1), (7, 0), (7, 1)):
                g, b = u_gemm(pn, mm_)
                act(uT[:, g, :], ps[:, b, :], AF.Gelu_apprx_tanh, [("ps", b)], [("u", g)], touch=[(u_region(g), tg("u"))])
            for g in range(4, 8):
                act(uT[:, g, :], uT[:, g, :], AF.Gelu_apprx_tanh, [("u", g)], [("u", g)], touch=[(u_region(g), tg("u"))])
            for g in range(8):
                b = bank()
                for p in mixed:
                    cs = slice(128 * p, 128 * p + 128)
                    mm(ps[:, b, cs], vn[:, p, 128 * g:128 * g + 128], wsT[:, 128 * g:128 * g + 128], True, False,
                       [("vn", p, g), "wsT"], [("ps", b)])
                    mm(ps[:, b, cs], ones[0:2, :], bbias[0:2, 128 * g:128 * g + 128], False, True,
                       ["ones", ("bhi", g // 4), ("blo", g // 4)], [("ps", b)])
                lo, hi = 128 * mixed[0], 128 * mixed[-1] + 128
                tt("dve", anT[:, g, lo:hi], ps[:, b, lo:hi], uT[:, g, lo:hi], ALU.mult, [("ps", b), ("u", g)], [("gm", g)],
                   touch=[("S1b", tg("gm")), (u_region(g), tg("u"))])
            AN_K = [("gm", g) for g in range(8)]
            if isdbg:
                dbg("gm", S1[:, 8192:16384], AN_K, [128, 8192], touch=[("S1b", tg("gm"))])
            branch_norm(gg, "gg", 8, "gm")
            MIX_ALL = [("mix", h) for h in range(16)]
            if isdbg:
                dbg("mixT", S1[:, 16384:24576], MIX_ALL, [128, 8192], touch=[("S1c", tg("mix"))])

            bN = 7
            pend_n = None
            for pn in range(8):
                wv, wk = wnext("wout", pn)
                for mm_ in range(2):
                    m = 2 * pn + mm_
                    b = bank()
                    for kc in range(16):
                        mm(ps[:, b, :], wv[:, kc, 128 * mm_:128 * mm_ + 128], mixT[:, kc, :], kc == 0, kc == 15,
                           [wk, ("mix", kc)], [("ps", b)], touch=[("S1c", tg("mix"))])
                    tt("dve", X1[:, m, XS:XS + 512], X1[:, m, XS:XS + 512], ps[:, b, :], ALU.add,
                       [("ps", b), ("x1", m)], [("x1", m)])
                    sq = sq_n[0] % 2
                    sq_n[0] += 1
                    act(sqring[sq][:, 0:512], X1[:, m, XS:XS + 512], AF.Square, [("x1", m)], [("sq", sq)])
                    if pend_n is not None:
                        pm, psq = pend_n
                        mm(ps[:, bN, :], ones[:], sqring[psq][:, 0:512], pm == 0, False, ["ones", ("sq", psq)], [("ps", bN)])
                    pend_n = (m, sq)
            pm, psq = pend_n
            mm(ps[:, bN, :], ones[:], sqring[psq][:, 0:512], False, True, ["ones", ("sq", psq)], [("ps", bN)])
            rms_finish(bN, 512, D, rstd2[:], ["rstd2"])
            if j == 0:
                R_("dve", lambda e: e.tensor_scalar(out=rstd2[:, 256:384], in0=rstd2[:, 256:384], scalar1=kvalid[:, 2:3],
                                                   scalar2=None, op0=ALU.mult), ["rstd2", "kvalid"], ["rstd2"])
            if j == NST - 1:
                R_("dve", lambda e: e.tensor_scalar(out=rstd2[:, 384:512], in0=rstd2[:, 384:512], scalar1=kvalid[:, 35:36],
                                                   scalar2=None, op0=ALU.mult), ["rstd2", "kvalid"], ["rstd2"])
            for m in range(16):
                stt("dve", h2win[:, m, HS:HS + 512], X1[:, m, XS:XS + 512], g2[:, m:m + 1],
                    rstd2[:], ALU.mult, ALU.mult, [("x1", m), "g2", "rstd2"], [("h2", m)], touch=[("hreg", tg("h2win"))])
            H2ALL = [("h2", m) for m in range(16)]
            if j > 0:
                cp(PL, h2win[:, :, HS - S_j - 1:HS], h2st[:, :, 0:S_j + 1], ["h2st"], ["h2s"],
                   touch=[("hreg", tg("h2win"))])
            if j < NST - 1:
                cp(PL, h2st[:, :, 0:S_n + 1], h2win[:, :, HS + 512 - S_n - 1:HS + 512], H2ALL, ["h2st"],
                   touch=[("hreg", tg("h2win"))])
            if isdbg:
                dbg("x1", X1[:, :, :].rearrange("p a b -> p (a b)"), [("x1", m) for m in range(16)], [128, 16 * (XS + 512)])

            na = n_out + 2
            ca = HS + (s_j - 1 - c0)
            for i in range(44):
                wv, wk = wnext("wup", i)
                bG, bU = bank(), bank()
                for (bb, off) in ((bG, 0), (bU, 128)):
                    for kc in range(16):
                        mm(ps[:, bb, 0:na], wv[:, kc, off:off + 128], h2win[:, kc, ca:ca + na], kc == 0, kc == 15,
                           [wk, ("h2", kc), "h2s"], [("ps", bb)], touch=[("hreg", tg("h2win"))])
                t1 = ct[(2 * i) % 4]
                t2 = ct[(2 * i + 1) % 4]
                k1 = CTK[(2 * i) % 4]
                k2 = CTK[(2 * i + 1) % 4]
                for (bb, tt_, kk, ch) in ((bG, t1, k1, i), (bU, t2, k2, 44 + i)):
                    tt_ = tt_ if (2 * i) % 4 == 0 else tt_
                    act(tt_[:, 0:n_out], ps[:, bb, 1:n_out + 1], AF.Identity, [("ps", bb), "convw", "convb"], [kk],
                        bias=convb[:, ch:ch + 1], scale=convw[:, 88 + ch:88 + ch + 1])
                    stt("dve", tt_[:, 0:n_out], ps[:, bb, 0:n_out], convw[:, ch:ch + 1], tt_[:, 0:n_out],
                        ALU.mult, ALU.add, [("ps", bb), kk, "convw"], [kk])
                    stt("dve", tt_[:, 0:n_out], ps[:, bb, 2:n_out + 2], convw[:, 176 + ch:176 + ch + 1], tt_[:, 0:n_out],
                        ALU.mult, ALU.add, [("ps", bb), kk, "convw"], [kk])
                act(t1[:, 0:n_out], t1[:, 0:n_out], AF.Silu, [k1], [k1])
                rg = "S1a0" if i < 8 else ("S1a1" if i < 16 else ("S1b" if i < 32 else "S1c"))
                tt(PL, actT[:, i, 0:n_out], t1[:, 0:n_out], t2[:, 0:n_out], ALU.mult, [k1, k2], [("act", i)],
                   touch=[(rg, tg("act"))])
            if isdbg:
                dbg("actT", S1[:, 0:44 * 512], [("act", i) for i in range(44)], [128, 44 * 512], touch=[("S1a0", tg("act")), ("S1a1", tg("act")), ("S1b", tg("act")), ("S1c", tg("act"))])

            cx = XS + (s_j - c0)
            bF = 7
            pend_f = None
            nrot[0] = 5
            for m in range(16):
                b = bank()
                for half in range(2):
                    wv, wk = wnext("wdn", 2 * m + half)
                    for kc in range(22):
                        ch = 22 * half + kc
                        rg = "S1a0" if ch < 8 else ("S1a1" if ch < 16 else ("S1b" if ch < 32 else "S1c"))
                        mm(ps[:, b, 0:n_out], wv[:, kc, :], actT[:, ch, 0:n_out], ch == 0, ch == 43,
                           [wk, ("act", ch)], [("ps", b)], touch=[(rg, tg("act"))])
                keys = [("x1", m)] + (["x1s"] if j > 0 else [])
                tt("dve", X1[:, m, cx:cx + n_out], X1[:, m, cx:cx + n_out], ps[:, b, 0:n_out], ALU.add,
                   [("ps", b)] + keys, keys)
                sq = sq_n[0] % 2
                sq_n[0] += 1
                act(sqring[sq][:, 0:n_out], X1[:, m, cx:cx + n_out], AF.Square, keys, [("sq", sq)])
                if j < NST - 1 and OPT_P1OV:
                    if m < 8:
                        p1_flush_mm()
                        p1_pass1(j + 1, 2 * m)
                        p1_pass1(j + 1, 2 * m + 1)
                    else:
                        if m == 8:
                            p1_flush_mm(final=True)
                            p1_finish()
                        p1_pass2(j + 1, 2 * (m - 8))
                        p1_pass2(j + 1, 2 * (m - 8) + 1)
                if pend_f is not None:
                    pm, psq = pend_f
                    mm(ps[:, bF, 0:n_out], ones[:], sqring[psq][:, 0:n_out], pm == 0, False, ["ones", ("sq", psq)], [("ps", bF)])
                pend_f = (m, sq)
            pm, psq = pend_f
            mm(ps[:, bF, 0:n_out], ones[:], sqring[psq][:, 0:n_out], False, True, ["ones", ("sq", psq)], [("ps", bF)])
            def make_tail(j=j, cx=cx, n_out=n_out, s_j=s_j, e_j=e_j, S_n=S_n):
                def tail():
                    rms_finish(7, n_out, D, rstd2[:, 0:n_out], ["rstd2"])
                    allk = [("x1", m) for m in range(16)] + ["x1s"]
                    for m in range(16):
                        keys = [("x1", m)] + (["x1s"] if j > 0 else [])
                        stt("dve", X1[:, m, cx:cx + n_out], X1[:, m, cx:cx + n_out], gf[:, m:m + 1],
                            rstd2[:, 0:n_out], ALU.mult, ALU.mult, keys + ["gf", "rstd2"], keys)

                def store():
                    allk = [("x1", m) for m in range(16)] + ["x1s"]
                    dma(outTv[:, :, s_j - OWN0:e_j - OWN0], X1[:, :, cx:cx + n_out], allk, ["outT"], ("out",))

                def stash():
                    allk = [("x1", m) for m in range(16)] + ["x1s"]
                    R_("sp", lambda e, a=X1[:, :, XS - S_n:XS], b_=X1[:, :, XS + 512 - S_n:XS + 512]:
                       e.dma_start(out=a, in_=b_, allow_slow_non_contiguous=True), allk, ["x1s"], ("x1st",))
                tail.stash = stash if j < NST - 1 else None
                tail.store = store
                return tail

            pending_tail[0] = make_tail()
            if j == NST - 1:
                pending_tail[0]()
                pending_tail[0].store()
                pending_tail[0] = None

        P.emit(final_waits=[("out",)] + [("dbg", n) for n in dbg_out])
    return nc, dbg_out


def _core_inputs(c, x, shared):
    bi, seg = divmod(c, NCORE // 2)
    s0 = seg * OWN
    g0 = s0 - OWN0
    xt = np.zeros((TOK, D), np.float32)
    lo, hi = max(g0, 0), min(g0 + TOK, SEQ)
    xt[lo - g0:hi - g0] = x[bi, lo:hi]
    xT = np.ascontiguousarray(xt.T).reshape(16, 128, TOK)
    kvalid = np.zeros((128, NB), np.float32)
    for B in range(NB):
        t = g0 + 128 * B
        kvalid[:, B] = 1.0 if (0 <= t < SEQ) else 0.0
    d = dict(shared)
    d["xT"] = xT
    d["kvalid"] = kvalid
    return d


def _shared_inputs(norm1_g, w_in, gmlp_v_norm_g, gmlp_ws, gmlp_b, attn_sink, attn_out_norm_g,
                   gmlp_out_norm_g, w_out, norm2_g, w_up, conv_w, conv_b, w_down, final_g):
    f = lambda a: np.ascontiguousarray(np.asarray(a, np.float32))
    col = lambda v, n: f(np.asarray(v, np.float32).reshape(n, 128).T)
    ki = np.arange(128)[:, None]
    qj = np.arange(128)[None, :]
    absrel = np.concatenate([np.abs((kb - 1) * 128 + ki - qj) for kb in range(3)], axis=1).astype(np.float32)
    mask01 = (absrel <= 128).astype(np.float32)
    cw = np.asarray(conv_w[0], np.float32)
    convw = np.concatenate([cw[k].reshape(88, 128).T for k in range(3)], axis=1)
    return {
        "w_in": f(w_in[0]), "w_out": f(w_out[0]), "w_up": f(w_up[0]), "w_down": f(w_down[0]),
        "g1": col(norm1_g[0], 16), "g2": col(norm2_g[0], 16), "gf": col(final_g, 16),
        "ga": col(attn_out_norm_g[0], 8), "gg": col(gmlp_out_norm_g[0], 8),
        "convw": f(convw), "convb": col(conv_b[0], 88),
        "sinkrep": f(np.broadcast_to(np.asarray(attn_sink[0], np.float32)[None, :], (128, 8))),
        "gvg": f(np.broadcast_to(np.asarray(gmlp_v_norm_g[0], np.float32)[None, :], (128, 1024))),
        "wsT": f(np.transpose(np.asarray(gmlp_ws[0], np.float32), (2, 0, 1)).reshape(128, 1024)),
        "brow": f(np.asarray(gmlp_b[0], np.float32).reshape(1, 1024)),
        "absrel": f(absrel), "mask01": f(mask01),
    }


_NC_CACHE = {}


def kernel(x, norm1_g, w_in, gmlp_v_norm_g, gmlp_ws, gmlp_b, attn_sink, attn_out_norm_g,
           gmlp_out_norm_g, w_out, norm2_g, w_up, conv_w, conv_b, w_down, final_g, _debug=None):
    x = np.asarray(x, np.float32)
    shared = _shared_inputs(norm1_g, w_in, gmlp_v_norm_g, gmlp_ws, gmlp_b, attn_sink, attn_out_norm_g,
                            gmlp_out_norm_g, w_out, norm2_g, w_up, conv_w, conv_b, w_down, final_g)
    in_maps = [_core_inputs(c, x, shared) for c in range(NCORE)]
    nc, dbg_out = build_nc(_debug)
    res = run_bass_kernel_spmd(nc, in_maps, core_ids=list(range(NCORE)))
    out = np.empty((2, SEQ, D), np.float32)
    for c in range(NCORE):
        bi, seg = divmod(c, NCORE // 2)
        o = np.asarray(res.results[c]["outT"]).reshape(D, OWN)
        out[bi, seg * OWN:(seg + 1) * OWN] = o.T
    if _debug is not None:
        return out, [{n: np.asarray(r["dbg_" + n]) for n in dbg_out} for r in res.results]
    return out
```
